# Optimizing a Trainium2 kernel written in Bass

```python
import jax, jax.numpy as jnp
from jax import lax
import numpy as np

D_MODEL = 2048
BATCH = 8
SEQ = 2048
DEPTH = 1

MEM_LEN = 256
HEAD_DIM = 128
CONV_WIDTH = D_MODEL // 2
ATTN_WIDTH = D_MODEL - CONV_WIDTH
N_ATTN_HEADS = ATTN_WIDTH // HEAD_DIM
CONV_KERNEL = 31
MOBA_BLOCK = 256
MOBA_TOPK = 3
MOBA_Q_CHUNK = 16
N_XATTN_HEADS = 4
XATTN_HEAD_DIM = 128
XATTN_WIDTH = N_XATTN_HEADS * XATTN_HEAD_DIM
D_FF = 5632
ROPE_THETA = 10000.0
RMS_EPS = 1e-6
LN_EPS = 1e-5
FFN_RES_SCALE = 0.5
IN_PROJ_WIDTH = 2 * CONV_WIDTH + 3 * ATTN_WIDTH
NEG_INF = -1e30

kernel_name = 'hybrid_conformer_moba_macaron_layer'


def rms_norm(x, g):
    xf = x.astype(jnp.float32)
    y = xf * lax.rsqrt(jnp.mean(xf * xf, axis=-1, keepdims=True) + RMS_EPS)
    return (y * g.astype(jnp.float32)).astype(x.dtype)


def layer_norm(x, g, b):
    xf = x.astype(jnp.float32)
    mu = jnp.mean(xf, axis=-1, keepdims=True)
    xc = xf - mu
    var = jnp.mean(xc * xc, axis=-1, keepdims=True)
    y = xc * lax.rsqrt(var + LN_EPS) * g.astype(jnp.float32) + b.astype(jnp.float32)
    return y.astype(x.dtype)


def swiglu_ffn(x, w_gu, w_down):
    gate, up = jnp.split(x @ w_gu, 2, axis=-1)
    return (jax.nn.silu(gate) * up) @ w_down


def rope_tables(seq, dim):
    inv_freq = 1.0 / (ROPE_THETA ** (jnp.arange(0, dim, 2, dtype=jnp.float32) / dim))
    ang = jnp.arange(seq, dtype=jnp.float32)[:, None] * inv_freq[None, :]
    return jnp.cos(ang), jnp.sin(ang)


def apply_rope(x, cos, sin):
    x1, x2 = jnp.split(x.astype(jnp.float32), 2, axis=-1)
    c, s = cos[None, None], sin[None, None]
    return jnp.concatenate([x1 * c - x2 * s, x2 * c + x1 * s], axis=-1).astype(x.dtype)


def conformer_conv_group(a, g, w_dw, b_dw, ln_g, ln_b):
    y = a * jax.nn.sigmoid(g)
    y = lax.conv_general_dilated(
        y, w_dw[:, None, :].astype(y.dtype), window_strides=(1,),
        padding=[(CONV_KERNEL - 1, 0)],
        dimension_numbers=('NWC', 'WIO', 'NWC'),
        feature_group_count=CONV_WIDTH) + b_dw
    y = layer_norm(y, ln_g, ln_b)
    return jax.nn.silu(y)


def moba_attention(q, k, v):
    B_, H_, S_, D_ = q.shape
    n_blocks = -(-S_ // MOBA_BLOCK)
    s_pad = n_blocks * MOBA_BLOCK
    pad = [(0, 0), (0, 0), (0, s_pad - S_), (0, 0)]
    q, k, v = jnp.pad(q, pad), jnp.pad(k, pad), jnp.pad(v, pad)
    kb = k.reshape(B_, H_, n_blocks, MOBA_BLOCK, D_)
    vb = v.reshape(B_, H_, n_blocks, MOBA_BLOCK, D_)
    k_mean = jnp.mean(kb.astype(jnp.float32), axis=3)
    top_k = min(MOBA_TOPK, n_blocks)
    scale = D_ ** -0.5
    n_chunks = s_pad // MOBA_Q_CHUNK
    q_chunks = q.reshape(B_, H_, n_chunks, MOBA_Q_CHUNK, D_).transpose(2, 0, 1, 3, 4)
    b_idx = jnp.arange(B_)[:, None, None, None]
    h_idx = jnp.arange(H_)[None, :, None, None]
    blk_ids = jnp.arange(n_blocks)
    sel_len = top_k * MOBA_BLOCK

    def one_chunk(args):
        c, q_c = args
        q_pos = c * MOBA_Q_CHUNK + jnp.arange(MOBA_Q_CHUNK)
        own = (c * MOBA_Q_CHUNK) // MOBA_BLOCK
        qf = q_c.astype(jnp.float32)
        gate = jnp.einsum('bhqd,bhnd->bhqn', qf, k_mean)
        gate = jnp.where(blk_ids < own, gate, NEG_INF)
        _, idx = lax.top_k(gate, top_k)
        valid = jnp.arange(top_k) < own
        k_sel = kb[b_idx, h_idx, idx].astype(jnp.float32)
        v_sel = vb[b_idx, h_idx, idx].astype(jnp.float32)
        s_sel = jnp.einsum('bhqd,bhqknd->bhqkn', qf, k_sel) * scale
        s_sel = jnp.where(valid[:, None], s_sel, NEG_INF).reshape(B_, H_, MOBA_Q_CHUNK, sel_len)
        k_own = lax.dynamic_index_in_dim(kb, own, axis=2, keepdims=False).astype(jnp.float32)
        v_own = lax.dynamic_index_in_dim(vb, own, axis=2, keepdims=False).astype(jnp.float32)
        s_own = jnp.einsum('bhqd,bhnd->bhqn', qf, k_own) * scale
        k_pos = own * MOBA_BLOCK + jnp.arange(MOBA_BLOCK)
        s_own = jnp.where(k_pos[None, :] <= q_pos[:, None], s_own, NEG_INF)
        p = jax.nn.softmax(jnp.concatenate([s_sel, s_own], axis=-1), axis=-1)
        p_sel = p[..., :sel_len].reshape(B_, H_, MOBA_Q_CHUNK, top_k, MOBA_BLOCK)
        out = (jnp.einsum('bhqkn,bhqknd->bhqd', p_sel, v_sel)
               + jnp.einsum('bhqn,bhnd->bhqd', p[..., sel_len:], v_own))
        return out.astype(q_c.dtype)

    out = lax.map(one_chunk, (jnp.arange(n_chunks), q_chunks))
    out = out.transpose(1, 2, 0, 3, 4).reshape(B_, H_, s_pad, D_)
    return out[:, :, :S_]


def memory_cross_attention(u, mem_n, w_q, w_kv, w_o):
    B_, S_, _ = u.shape
    M_ = mem_n.shape[1]
    q = (u @ w_q).reshape(B_, S_, N_XATTN_HEADS, XATTN_HEAD_DIM)
    k, v = jnp.split(mem_n @ w_kv, 2, axis=-1)
    k = k.reshape(B_, M_, N_XATTN_HEADS, XATTN_HEAD_DIM)
    v = v.reshape(B_, M_, N_XATTN_HEADS, XATTN_HEAD_DIM)
    s = jnp.einsum('bshd,bmhd->bhsm', q.astype(jnp.float32), k.astype(jnp.float32)) * (XATTN_HEAD_DIM ** -0.5)
    p = jax.nn.softmax(s, axis=-1)
    o = jnp.einsum('bhsm,bmhd->bshd', p, v.astype(jnp.float32)).astype(u.dtype)
    return o.reshape(B_, S_, XATTN_WIDTH) @ w_o


def setup_inputs(seed: int = 0) -> dict:
    key = jax.random.key(seed)
    ks = jax.random.split(key, 26)
    L = DEPTH

    def w(k, shape, fan_in):
        return jax.random.normal(k, shape, jnp.float32) * (fan_in ** -0.5)

    def gain(k, n):
        return 1.0 + 0.05 * jax.random.normal(k, (L, n), jnp.float32)

    def small(k, shape):
        return 0.02 * jax.random.normal(k, shape, jnp.float32)

    return {
        'x': jax.random.normal(ks[0], (BATCH, SEQ, D_MODEL), jnp.float32),
        'mem': jax.random.normal(ks[1], (BATCH, MEM_LEN, D_MODEL), jnp.float32),
        'ffn1_pre_g': gain(ks[2], D_MODEL),
        'ffn1_w_gu': w(ks[3], (L, D_MODEL, 2 * D_FF), D_MODEL),
        'ffn1_w_down': w(ks[4], (L, D_FF, D_MODEL), D_FF),
        'ffn1_post_g': gain(ks[5], D_MODEL),
        'mix_pre_g': gain(ks[6], D_MODEL),
        'w_in': w(ks[7], (L, D_MODEL, IN_PROJ_WIDTH), D_MODEL),
        'conv_w_dw': w(ks[8], (L, CONV_KERNEL, CONV_WIDTH), CONV_KERNEL),
        'conv_b_dw': small(ks[9], (L, CONV_WIDTH)),
        'conv_ln_g': gain(ks[10], CONV_WIDTH),
        'conv_ln_b': small(ks[11], (L, CONV_WIDTH)),
        'w_out': w(ks[12], (L, CONV_WIDTH + ATTN_WIDTH, D_MODEL), CONV_WIDTH + ATTN_WIDTH),
        'mix_post_g': gain(ks[13], D_MODEL),
        'xattn_pre_g': gain(ks[14], D_MODEL),
        'mem_g': gain(ks[15], D_MODEL),
        'xattn_w_q': w(ks[16], (L, D_MODEL, XATTN_WIDTH), D_MODEL),
        'xattn_w_kv': w(ks[17], (L, D_MODEL, 2 * XATTN_WIDTH), D_MODEL),
        'xattn_w_o': w(ks[18], (L, XATTN_WIDTH, D_MODEL), XATTN_WIDTH),
        'xattn_post_g': gain(ks[19], D_MODEL),
        'ffn2_pre_g': gain(ks[20], D_MODEL),
        'ffn2_w_gu': w(ks[21], (L, D_MODEL, 2 * D_FF), D_MODEL),
        'ffn2_w_down': w(ks[22], (L, D_FF, D_MODEL), D_FF),
        'ffn2_post_g': gain(ks[23], D_MODEL),
    }


def reference(x, mem, ffn1_pre_g, ffn1_w_gu, ffn1_w_down, ffn1_post_g, mix_pre_g, w_in,
              conv_w_dw, conv_b_dw, conv_ln_g, conv_ln_b, w_out, mix_post_g,
              xattn_pre_g, mem_g, xattn_w_q, xattn_w_kv, xattn_w_o, xattn_post_g,
              ffn2_pre_g, ffn2_w_gu, ffn2_w_down, ffn2_post_g):
    B_, S_, _ = x.shape
    cos, sin = rope_tables(S_, HEAD_DIM)
    split_at = [CONV_WIDTH, 2 * CONV_WIDTH, 2 * CONV_WIDTH + ATTN_WIDTH, 2 * CONV_WIDTH + 2 * ATTN_WIDTH]
    h = x
    for l in range(DEPTH):
        f = swiglu_ffn(rms_norm(h, ffn1_pre_g[l]), ffn1_w_gu[l], ffn1_w_down[l])
        h = h + FFN_RES_SCALE * rms_norm(f, ffn1_post_g[l])

        u = rms_norm(h, mix_pre_g[l])
        z = u @ w_in[l]
        conv_a, conv_g, q, k, v = jnp.split(z, split_at, axis=-1)
        conv_out = conformer_conv_group(conv_a, conv_g, conv_w_dw[l], conv_b_dw[l],
                                        conv_ln_g[l], conv_ln_b[l])
        heads = lambda t: t.reshape(B_, S_, N_ATTN_HEADS, HEAD_DIM).transpose(0, 2, 1, 3)
        q = apply_rope(heads(q), cos, sin)
        k = apply_rope(heads(k), cos, sin)
        attn = moba_attention(q, k, heads(v))
        attn = attn.transpose(0, 2, 1, 3).reshape(B_, S_, ATTN_WIDTH)
        y = jnp.concatenate([conv_out, attn.astype(conv_out.dtype)], axis=-1) @ w_out[l]
        h = h + rms_norm(y, mix_post_g[l])

        c = memory_cross_attention(rms_norm(h, xattn_pre_g[l]), rms_norm(mem, mem_g[l]),
                                   xattn_w_q[l], xattn_w_kv[l], xattn_w_o[l])
        h = h + rms_norm(c, xattn_post_g[l])

        f = swiglu_ffn(rms_norm(h, ffn2_pre_g[l]), ffn2_w_gu[l], ffn2_w_down[l])
        h = h + FFN_RES_SCALE * rms_norm(f, ffn2_post_g[l])
    return h
```

```python
import numpy as np
from contextlib import ExitStack
import concourse.bass as bass
import concourse.mybir as mybir
from concourse.bass_utils import run_bass_kernel_spmd

F32 = mybir.dt.float32
BF16 = mybir.dt.bfloat16
AF = mybir.ActivationFunctionType
ALU = mybir.AluOpType
AX = mybir.AxisListType

D = 2048
T = 2048
DFF = 5632
NFF = DFF // 128
MEM = 256
NEG = -30000.0
RMS_EPS = 1e-6
LN_EPS = 1e-5

GV_FFN1_PRE, GV_FFN1_POST, GV_MIX_PRE, GV_MIX_POST = 0, 16, 32, 48
GV_X_PRE, GV_MEM, GV_X_POST, GV_FFN2_PRE, GV_FFN2_POST = 64, 80, 96, 112, 128
GV_CONV_B, GV_LN_G, GV_LN_B, GV_CONV_W = 144, 152, 160, 168
NGV = 168 + 8 * 31


class Sched:
    ENG = ('pe', 'act', 'dve', 'pool', 'sp')

    def __init__(self, nc, es, name, ndma=6):
        self.nc = nc
        self.ops = {e: [] for e in self.ENG}
        self.cnt = {e: 0 for e in self.ENG}
        self.csem = {e: nc.alloc_semaphore(name=f"{name}_c_{e}") for e in ('pe', 'act', 'dve', 'pool')}
        self.dsem = {q: [nc.alloc_semaphore(name=f"{name}_d_{q}{i}") for i in range(ndma)]
                     for q in ('sp', 'pool')}
        self.dcnt = {q: 0 for q in self.dsem}
        self.ndma = ndma
        self.lw = {}
        self.rd = {}
        self.waited = {e: {} for e in self.ENG}

    def _sem(self, skey):
        if skey[0] == 'c':
            return self.csem[skey[1]]
        return self.dsem[skey[1]][skey[2]]

    def _deps(self, eng, reads, writes):
        need = {}

        def add(skey, val, kind):
            if skey[0] == 'c' and skey[1] == eng:
                if eng == 'pe' or kind == 'war':
                    return
            if need.get(skey, 0) < val:
                need[skey] = val

        for k in reads:
            t = self.lw.get(k)
            if t is not None:
                add(t[0], t[1], 'raw')
            if k[0] == 'ps':
                for skey, val in self.rd.get(k, {}).items():
                    if skey[0] == 'c' and skey[1] != eng:
                        add(skey, val, 'rar')
        for k in writes:
            t = self.lw.get(k)
            if t is not None:
                add(t[0], t[1], 'waw')
            for skey, val in self.rd.get(k, {}).items():
                add(skey, val, 'war')
        waits = []
        w = self.waited[eng]
        for skey, val in need.items():
            if w.get(skey, 0) < val:
                w[skey] = val
                waits.append((self._sem(skey), val))
        return waits

    def _update(self, tok, reads, writes):
        for k in writes:
            self.lw[k] = tok
            self.rd[k] = {}
        for k in reads:
            d = self.rd.setdefault(k, {})
            if d.get(tok[0], 0) < tok[1]:
                d[tok[0]] = tok[1]

    def op(self, eng, fn, reads=(), writes=()):
        waits = self._deps(eng, reads, writes)
        self.cnt[eng] += 1
        tok = (('c', eng), self.cnt[eng])
        self.ops[eng].append((waits, fn, (self.csem[eng], 1)))
        self._update(tok, reads, writes)

    def dma(self, q, out, in_, reads=(), writes=()):
        waits = self._deps(q, reads, writes)
        i = self.dcnt[q]
        self.dcnt[q] += 1
        slot, gen = i % self.ndma, i // self.ndma
        skey = ('d', q, slot)
        if gen > 0 and self.waited[q].get(skey, 0) < 16 * gen:
            self.waited[q][skey] = 16 * gen
            waits.append((self.dsem[q][slot], 16 * gen))
        tok = (skey, 16 * (gen + 1))
        self.ops[q].append((waits, lambda e, o=out, i_=in_: e.dma_start(out=o, in_=i_),
                            (self.dsem[q][slot], 16)))
        self._update(tok, reads, writes)

    def emit(self):
        final = []
        for q in self.dsem:
            n = self.dcnt[q]
            for slot in range(min(n, self.ndma)):
                gens = (n - 1 - slot) // self.ndma + 1
                final.append((self.dsem[q][slot], 16 * gens))
        ops = self.ops

        def run(e, name, extra=()):
            for waits, fn, inc in ops[name]:
                for s, v in waits:
                    e.wait_ge(s, v)
                fn(e).then_inc(*inc)
            for s, v in extra:
                e.wait_ge(s, v)

        with self.nc.Block() as blk:
            blk.tensor(lambda e: run(e, 'pe'))
            blk.scalar(lambda e: run(e, 'act'))
            blk.vector(lambda e: run(e, 'dve'))
            blk.gpsimd(lambda e: run(e, 'pool'))
            blk.sync(lambda e: run(e, 'sp', final))
        allsems = list(self.csem.values()) + [s for l in self.dsem.values() for s in l]
        self.nc.clear_and_free_semaphores(allsems)
        self.nc.all_engine_barrier()


class RR:
    def __init__(self, n, off=0):
        self.n = n
        self.off = off
        self.i = -1

    def __call__(self):
        self.i += 1
        return self.off + self.i % self.n


def gk(name, s, ng):
    return [(name, s, g) for g in range(ng)]


def gen_norm(S, G, hin, hname, t0, Tn, gcol, xn, stg, stg_rr, sqp, sq_rr, rstd, pbase=0, lag=0, resident=None):
    ps, ones_bf, gv = G['ps'], G['ones_bf'], G['gv']
    ng = Tn // 512
    pend = []
    if resident is not None:
        for kc in range(16):
            S.dma('sp', resident[:, kc, 0:Tn], hin[kc, :, t0:t0 + Tn], reads=[(hname, kc)], writes=[('hres', kc)])
    for kc in range(16):
        if resident is None:
            s = stg_rr()
            S.dma('sp', stg[:, s, 0:Tn], hin[kc, :, t0:t0 + Tn], reads=[(hname, kc)], writes=gk('stg', s, 4))
            src_ap, src_k = stg[:, s, 0:Tn], gk('stg', s, 4)
        else:
            src_ap, src_k = resident[:, kc, 0:Tn], [('hres', kc)]
        q = sq_rr()
        S.op('act', lambda e, src_ap=src_ap, q=q: e.activation(out=sqp[:, q, 0:Tn], in_=src_ap, func=AF.Square),
             reads=src_k, writes=gk('sq', q, 4))

        def f(e, q=q, kc=kc):
            for g in range(ng):
                ins = e.matmul(ps[pbase + g][:, :], ones_bf[:, :], sqp[:, q, g * 512:(g + 1) * 512],
                               start=(kc == 0), stop=(kc == 15))
            return ins
        pend.append((f, gk('sq', q, 4)))
        if len(pend) > lag:
            fn_, r_ = pend.pop(0)
            S.op('pe', fn_, reads=r_, writes=[('ps', pbase + g) for g in range(ng)])
        yield
    for fn_, r_ in pend:
        S.op('pe', fn_, reads=r_, writes=[('ps', pbase + g) for g in range(ng)])
    emit_rstd(S, G, rstd, Tn, [pbase + g for g in range(ng)], 1.0 / D, RMS_EPS)
    yield
    for kc in range(16):
        if resident is None:
            s = stg_rr()
            S.dma('sp', stg[:, s, 0:Tn], hin[kc, :, t0:t0 + Tn], reads=[(hname, kc)], writes=gk('stg', s, 4))
            src_ap, src_k = stg[:, s, 0:Tn], gk('stg', s, 4)
        else:
            src_ap, src_k = resident[:, kc, 0:Tn], [('hres', kc)]
        S.op('dve', lambda e, src_ap=src_ap, kc=kc: e.scalar_tensor_tensor(
            out=xn[:, kc, 0:Tn], in0=src_ap, scalar=gv[:, gcol + kc:gcol + kc + 1],
            in1=rstd[:, 0:Tn], op0=ALU.mult, op1=ALU.mult),
            reads=src_k + gk('rstd', 0, 4), writes=[('xn', kc)])
        yield


def emit_norm(*a, **k):
    for _ in gen_norm(*a, **k):
        pass


def merge_side_first(main, side):
    while True:
        next(side, None)
        try:
            next(main)
        except StopIteration:
            break
    for _ in side:
        pass


def merge(main, side, per_step):
    for _ in main:
        for _i in range(per_step):
            next(side, None)
    for _ in side:
        pass


def emit_rstd(S, G, rstd, Tn, banks, scale, eps, key='rstd'):
    ps = G['ps']
    for g, b in enumerate(banks):
        S.op('dve', lambda e, g=g, b=b: e.tensor_scalar(
            out=rstd[:, g * 512:(g + 1) * 512], in0=ps[b][:, :], scalar1=scale, scalar2=eps,
            op0=ALU.mult, op1=ALU.add), reads=[('ps', b)], writes=[(key, 0, g)])
        S.op('act', lambda e, g=g: e.activation(out=rstd[:, g * 512:(g + 1) * 512],
                                                in_=rstd[:, g * 512:(g + 1) * 512], func=AF.Sqrt),
             reads=[(key, 0, g)], writes=[(key, 0, g)])
        S.op('dve', lambda e, g=g: e.reciprocal(out=rstd[:, g * 512:(g + 1) * 512],
                                                in_=rstd[:, g * 512:(g + 1) * 512]),
             reads=[(key, 0, g)], writes=[(key, 0, g)])


def gen_down_C(S, G, hf, rhs, rkeys, KC, wdn_sb, wdn_ensure, widx0, hin, hname, hout, oname, gpost, res_scale,
               stg, stg_rr, sqp, sq_rr, rstd, fscr, fsb=None, hsb=None, rkey='rstd'):
    ps, ones_bf, gv = G['ps'], G['ones_bf'], G['gv']
    t0 = hf * 1024
    pend = []
    for c in range(16):
        if hsb is not None:
            S.dma('sp', hsb[:, c, :], hin[c, :, t0:t0 + 1024], reads=[(hname, c)], writes=[('hsb', c)])
        gi = widx0 + c
        wdn_ensure(gi + 1)
        b = gi % 2
        if fsb is None:
            s = stg_rr()
        q = sq_rr()
        pend_new = []
        for tg in (0, 1):
            bank = (c % 2) * 2 + tg
            sl = slice(tg * 512, (tg + 1) * 512)
            if fsb is None:
                fdst, fkey = stg[:, s, sl], ('stg', s, tg)
            else:
                fdst, fkey = fsb[:, c, sl], ('fsb', c, tg)

            def f(e, b=b, tg=tg, bank=bank):
                for kc in range(KC):
                    ins = e.matmul(ps[bank][:, :], wdn_sb[:, b, kc, :], rhs(kc, tg),
                                   start=(kc == 0), stop=(kc == KC - 1))
                return ins
            S.op('pe', f, reads=[('wdn', b)] + rkeys(tg), writes=[('ps', bank)])
            S.op('dve', lambda e, fdst=fdst, bank=bank: e.tensor_copy(out=fdst, in_=ps[bank][:, :]),
                 reads=[('ps', bank)], writes=[fkey])
            S.op('act', lambda e, fdst=fdst, q=q, sl=sl: e.activation(out=sqp[:, q, sl], in_=fdst, func=AF.Square),
                 reads=[fkey], writes=[('sq', q, tg)])
            pend_new.append((lambda e, q=q, tg=tg, c=c, sl=sl: e.matmul(
                ps[6 + tg][:, :], ones_bf[:, :], sqp[:, q, sl],
                start=(c == 0), stop=(c == 15)), [('sq', q, tg)], [('ps', 6 + tg)]))
        for fn_, r_, w_ in pend:
            S.op('pe', fn_, reads=r_, writes=w_)
        pend = pend_new
        if fsb is None:
            S.dma('sp', fscr[c], stg[:, s, 0:1024], reads=gk('stg', s, 2), writes=[('fscr', c)])
        yield
    for fn_, r_, w_ in pend:
        S.op('pe', fn_, reads=r_, writes=w_)
    emit_rstd(S, G, rstd, 1024, [6, 7], 1.0 / D, RMS_EPS, key=rkey)


def gen_down_D(S, G, hf, rhs, rkeys, KC, wdn_sb, wdn_ensure, widx0, hin, hname, hout, oname, gpost, res_scale,
               stg, stg_rr, sqp, sq_rr, rstd, fscr, fsb=None, hsb=None, rkey='rstd', stage=None):
    gv = G['gv']
    t0 = hf * 1024
    slots = {}
    if fsb is None and stage is not None:
        for c in range(16):
            slots[c] = stage(c)
            S.dma('sp', slots[c][0], fscr[c], reads=[('fscr', c)], writes=slots[c][1])
        for c in range(6):
            S.dma('sp', slots[c][2], hin[c, :, t0:t0 + 1024], reads=[(hname, c)], writes=slots[c][3])
    for c in range(16):
        if fsb is None and stage is None:
            s1 = stg_rr()
            s2 = stg_rr()
            S.dma('sp', stg[:, s1, 0:1024], fscr[c], reads=[('fscr', c)], writes=gk('stg', s1, 4))
            S.dma('sp', stg[:, s2, 0:1024], hin[c, :, t0:t0 + 1024], reads=[(hname, c)], writes=gk('stg', s2, 4))
            fap, fkeys = stg[:, s1, 0:1024], gk('stg', s1, 4)
            hap, hkeys = stg[:, s2, 0:1024], gk('stg', s2, 4)
        elif fsb is None:
            fap, fkeys, hap, hkeys = slots[c]
        else:
            fap, fkeys = fsb[:, c, :], gk('fsb', c, 2)
            hap, hkeys = hsb[:, c, :], [('hsb', c)]
        S.op('dve', lambda e, fap=fap, c=c: e.scalar_tensor_tensor(
            out=fap, in0=fap, scalar=gv[:, gpost + c:gpost + c + 1],
            in1=rstd[:, 0:1024], op0=ALU.mult, op1=ALU.mult),
            reads=fkeys + gk(rkey, 0, 2), writes=fkeys)
        S.op('dve', lambda e, fap=fap, hap=hap: e.scalar_tensor_tensor(
            out=fap, in0=fap, scalar=res_scale, in1=hap, op0=ALU.mult, op1=ALU.add),
            reads=fkeys + hkeys, writes=fkeys)
        S.dma('sp', hout[c, :, t0:t0 + 1024], fap, reads=fkeys, writes=[(oname, c)])
        if slots and c + 6 < 16:
            S.dma('sp', slots[c + 6][2], hin[c + 6, :, t0:t0 + 1024], reads=[(hname, c + 6)], writes=slots[c + 6][3])
        yield


def emit_down(*a, **k):
    for _ in gen_down_C(*a, **k):
        pass
    for _ in gen_down_D(*a, **k):
        pass


def ffn_phase(nc, G, name, hin, hname, hout, oname, wgu, wdn, gpre, gpost):
    with ExitStack() as es:
        sb = lambda n, shp, dt: es.enter_context(nc.sbuf_tensor(f"{name}_{n}", shp, dt))
        xn = sb("xn", [128, 16, 1024], BF16)
        act = sb("act", [128, NFF, 1024], BF16)
        wgu_sb = sb("wgu", [128, 2, 16, 256], BF16)
        wdn_sb = sb("wdn", [128, 2, NFF, 128], BF16)
        stg = sb("stg", [128, 4, 1024], F32)
        sqp = sb("sqp", [128, 6, 1024], BF16)
        rstd = sb("rstd", [128, 1024], F32)
        rstdf = sb("rstdf", [128, 1024], F32)
        sil = sb("sil", [128, 2, 512], F32)
        fscr = nc.dram_tensor(f"{name}_fscr", [2, 16, 128, 1024], F32).ap()
        S = Sched(nc, es, name)
        ps = G['ps']
        stg_rr, sq_rr, sq_rr_c = RR(4), RR(4), RR(2, 4)
        st = {'gu': -1, 'dn': -1}

        def wgu_ensure(i):
            while st['gu'] < min(i, 2 * NFF - 1):
                st['gu'] += 1
                g = st['gu']
                S.dma('pool', wgu_sb[:, g % 2], wgu[g % NFF], writes=[('wgu', g % 2)])

        def wdn_ensure(i):
            while st['dn'] < min(i, 31):
                st['dn'] += 1
                g = st['dn']
                S.dma('pool', wdn_sb[:, g % 2], wdn[g % 16], writes=[('wdn', g % 2)])

        xkeys = [('xn', kc) for kc in range(16)]

        hres = act.bitcast(F32).reshape([128, NFF // 2, 1024])

        def gen_A(hf, pbase, lag):
            wgu_ensure(hf * NFF + 1)
            yield from gen_norm(S, G, hin, hname, hf * 1024, 1024, gpre, xn, stg, stg_rr, sqp, sq_rr, rstd,
                                pbase=pbase, lag=lag, resident=(hres if hf == 0 else None))

        def gen_B(hf):
            wdn_ensure(hf * 16 + 1)
            for j in range(NFF):
                gi = hf * NFF + j
                wgu_ensure(gi + 1)
                b = gi % 2
                base = (j % 2) * 4
                def f(e, b=b, base=base):
                    for which in (0, 1):
                        for tg in (0, 1):
                            bank = base + which * 2 + tg
                            for kc in range(16):
                                ins = e.matmul(ps[bank][:, :], wgu_sb[:, b, kc, which * 128:(which + 1) * 128],
                                               xn[:, kc, tg * 512:(tg + 1) * 512], start=(kc == 0), stop=(kc == 15))
                    return ins
                S.op('pe', f, reads=[('wgu', b)] + xkeys, writes=[('ps', base + i) for i in range(4)])
                for tg in (0, 1):
                    S.op('act', lambda e, tg=tg, base=base: e.activation(
                        out=sil[:, tg, :], in_=ps[base + tg][:, :], func=AF.Silu),
                        reads=[('ps', base + tg)], writes=[('sil', tg)])
                    S.op('dve', lambda e, tg=tg, base=base, j=j: e.tensor_tensor(
                        out=act[:, j, tg * 512:(tg + 1) * 512], in0=ps[base + 2 + tg][:, :], in1=sil[:, tg, :],
                        op=ALU.mult), reads=[('ps', base + 2 + tg), ('sil', tg)], writes=[('act', j, tg)])
                yield

        def dargs(hf):
            return (S, G, hf, lambda kc, tg: act[:, kc, tg * 512:(tg + 1) * 512],
                    lambda tg: [('act', j, tg) for j in range(NFF)], NFF, wdn_sb, wdn_ensure,
                    hf * 16, hin, hname, hout, oname, gpost, 0.5, stg, stg_rr, sqp, sq_rr_c, rstdf, fscr[hf])

        for _ in gen_A(0, 0, 2):
            pass
        for _ in gen_B(0):
            pass
        merge(gen_down_C(*dargs(0), rkey="rstdf"), gen_A(1, 4, 2), 2)
        merge(gen_B(1), gen_down_D(*dargs(0), rkey='rstdf'), 1)
        for _ in gen_down_C(*dargs(1), rkey='rstdf'):
            pass
        def stage(c):
            def akeys(slot):
                return [('act', 2 * slot + i, tg) for i in (0, 1) for tg in (0, 1)]
            return hres[:, c, :], akeys(c), hres[:, 16 + c % 6, :], akeys(16 + c % 6)
        for _ in gen_down_D(*dargs(1), rkey='rstdf', stage=stage):
            pass
        S.emit()


def emit_rstd_sb(S, buf, key, ng, eps):
    for g in range(ng):
        sl = slice(g * 512, (g + 1) * 512)
        S.op('dve', lambda e, sl=sl: e.tensor_scalar(out=buf[:, sl], in0=buf[:, sl], scalar1=eps, scalar2=None,
                                                      op0=ALU.add), reads=[(key, 0, g)], writes=[(key, 0, g)])
        S.op('act', lambda e, sl=sl: e.activation(out=buf[:, sl], in_=buf[:, sl], func=AF.Sqrt),
             reads=[(key, 0, g)], writes=[(key, 0, g)])
        S.op('dve', lambda e, sl=sl: e.reciprocal(out=buf[:, sl], in_=buf[:, sl]),
             reads=[(key, 0, g)], writes=[(key, 0, g)])


def down_only_phase(nc, G, name, src_loader, KC, wsrc, hin, hname, hout, oname, gpost, res_scale):
    with ExitStack() as es:
        sb = lambda n, shp, dt: es.enter_context(nc.sbuf_tensor(f"{name}_{n}", shp, dt))
        wdn_sb = sb("wdn", [128, 2, KC, 128], BF16)
        stg = None
        fsb = sb("fsb", [128, 16, 1024], F32)
        hsb = sb("hsb", [128, 16, 1024], F32)
        sqp = sb("sqp", [128, 2, 1024], BF16)
        rstd = sb("rstd", [128, 1024], F32)
        fscr = nc.dram_tensor(f"{name}_fscr", [16, 128, 1024], F32).ap()
        S = Sched(nc, es, name)
        stg_rr, sq_rr = RR(4), RR(2)
        st = {'dn': -1}

        def wdn_ensure(i):
            while st['dn'] < min(i, 31):
                st['dn'] += 1
                g = st['dn']
                S.dma('pool', wdn_sb[:, g % 2], wsrc[g % 16], writes=[('wdn', g % 2)])

        rstd2 = sb("rstd2", [128, 1024], F32)

        def dargs(hf, rhs, rkeys):
            return (S, G, hf, rhs, rkeys, KC, wdn_sb, wdn_ensure, hf * 16, hin, hname, hout, oname, gpost,
                    res_scale, stg, stg_rr, sqp, sq_rr, (rstd, rstd2)[hf], fscr)
        kw = [dict(fsb=fsb, hsb=hsb, rkey='rstd'), dict(fsb=fsb, hsb=hsb, rkey='rstd2')]
        wdn_ensure(1)
        a0 = dargs(0, *src_loader(S, es, 0))
        for _ in gen_down_C(*a0, **kw[0]):
            pass
        wdn_ensure(17)
        a1 = dargs(1, *src_loader(S, es, 1))
        merge_side_first(gen_down_C(*a1, **kw[1]), gen_down_D(*a0, **kw[0]))
        for _ in gen_down_D(*a1, **kw[1]):
            pass
        S.emit()


def norm_full_phase(nc, G, name, hin, hname, gcol, uT):
    ps, ones_bf, gv = G['ps'], G['ones_bf'], G['gv']
    with ExitStack() as es:
        sb = lambda n, shp, dt: es.enter_context(nc.sbuf_tensor(f"{name}_{n}", shp, dt))
        hs = sb("hs", [128, 2, 16, 512], F32)
        sqp = sb("sqp", [128, 4, 512], BF16)
        rstd = sb("rstd", [128, 2048], F32)
        S = Sched(nc, es, name)
        sq_rr = RR(4)

        def loads(g):
            for kc in range(16):
                S.dma('sp', hs[:, g % 2, kc, :], hin[kc, :, g * 512:(g + 1) * 512], reads=[(hname, kc)],
                      writes=[('hs', g % 2, kc)])
        loads(0)
        loads(1)
        for g in range(4):
            b = g % 2
            bank = g % 2
            pend = []
            for kc in range(16):
                q = sq_rr()
                S.op('act', lambda e, b=b, kc=kc, q=q: e.activation(out=sqp[:, q, :], in_=hs[:, b, kc, :],
                                                                    func=AF.Square),
                     reads=[('hs', b, kc)], writes=[('sq', q)])
                pend.append((lambda e, q=q, kc=kc, bank=bank: e.matmul(
                    ps[bank][:, :], ones_bf[:, :], sqp[:, q, :], start=(kc == 0), stop=(kc == 15)), [('sq', q)]))
                if len(pend) > 2:
                    fn_, r_ = pend.pop(0)
                    S.op('pe', fn_, reads=r_, writes=[('ps', bank)])
            for fn_, r_ in pend:
                S.op('pe', fn_, reads=r_, writes=[('ps', bank)])
            sl = slice(g * 512, (g + 1) * 512)
            S.op('dve', lambda e, sl=sl, bank=bank: e.tensor_scalar(
                out=rstd[:, sl], in0=ps[bank][:, :], scalar1=1.0 / D, scalar2=RMS_EPS, op0=ALU.mult, op1=ALU.add),
                reads=[('ps', bank)], writes=[('rstd', g)])
            S.op('act', lambda e, sl=sl: e.activation(out=rstd[:, sl], in_=rstd[:, sl], func=AF.Sqrt),
                 reads=[('rstd', g)], writes=[('rstd', g)])
            S.op('dve', lambda e, sl=sl: e.reciprocal(out=rstd[:, sl], in_=rstd[:, sl]),
                 reads=[('rstd', g)], writes=[('rstd', g)])
            for kc in range(16):
                S.op('dve', lambda e, b=b, kc=kc, sl=sl: e.scalar_tensor_tensor(
                    out=uT[:, kc, sl], in0=hs[:, b, kc, :], scalar=gv[:, gcol + kc:gcol + kc + 1],
                    in1=rstd[:, sl], op0=ALU.mult, op1=ALU.mult),
                    reads=[('hs', b, kc), ('rstd', g)], writes=[('xn', kc, g)])
            if g + 2 < 4:
                loads(g + 2)
        S.emit()


def mix_phase(nc, G, hin, hname, hout, oname, W):
    ps, ones_bf, gv = G['ps'], G['ones_bf'], G['gv']
    catscr = nc.dram_tensor("catscr", [16, 128, T], BF16).ap()
    win_t, win_v, wout_t, C = W['win_t'], W['win_v'], W['wout_t'], W['cst']
    ukeys = [('xn', kc) for kc in range(16)]
    _stop = ''
    with ExitStack() as ues:
        if _stop == 'only_m3':
            ues.close()
            return _mix_m3(nc, G, hin, hname, hout, oname, W, catscr)
        uT = ues.enter_context(nc.sbuf_tensor("mix_uT", [128, 16, T], BF16))
        norm_full_phase(nc, G, "m0", hin, hname, GV_MIX_PRE, uT)
        if _stop == 'm0':
            return

        with ExitStack() as es:
            sb = lambda n, shp, dt: es.enter_context(nc.sbuf_tensor(f"m1_{n}", shp, dt))
            y = sb("y", [128, 8, T], F32)
            glu = sb("glu", [128, 2, T + 30], BF16)
            diag = sb("diag", [128, 2, 31, 128], BF16)
            wc = sb("wc", [128, 2, 2, 16, 128], BF16)
            sig = sb("sig", [128, 2, 512], F32)
            ysq = sb("ysq", [128, 1024], F32)
            tmp = sb("tmp", [128, 1024], F32)
            mu = sb("mu", [128, 1024], F32)
            rsc = sb("rsc", [128, 1024], F32)
            ostg = sb("ostg", [128, 2, 1024], BF16)
            ones_f = sb("ones_f", [128, 128], F32)
            identb = sb("identb", [128, 128], BF16)
            S = Sched(nc, es, "m1")
            S.dma('pool', identb[:, :], C['ident'], writes=[('identb',)])
            S.op('dve', lambda e: e.memset(ones_f[:, :], 1.0), writes=[('ones_f',)])
            for b in range(2):
                S.op('dve', lambda e, b=b: e.memset(glu[:, b, 0:30], 0.0), writes=[('glupad', b)])
            st = {'w': -1}

            def wc_ensure(i):
                while st['w'] < min(i, 7):
                    st['w'] += 1
                    c = st['w']
                    S.dma('pool', wc[:, c % 2, 0], win_t[c], writes=[('wc', c % 2, 0)])
                    S.dma('pool', wc[:, c % 2, 1], win_t[8 + c], writes=[('wc', c % 2, 1)])

            def conv_part(cc):
                b = cc % 2
                for tg in range(4):
                    cbk = 4 + tg % 2

                    def f(e, b=b, tg=tg, cbk=cbk):
                        for j in range(31):
                            ins = e.matmul(ps[cbk][:, :], diag[:, b, j, :], glu[:, b, j + tg * 512:j + (tg + 1) * 512],
                                           start=(j == 0), stop=(j == 30))
                        return ins
                    S.op('pe', f, reads=[('diag', b), ('glupad', b)] + gk('glu', b, tg + 1), writes=[('ps', cbk)])
                    S.op('act', lambda e, cc=cc, tg=tg, cbk=cbk: e.activation(
                        out=y[:, cc, tg * 512:(tg + 1) * 512], in_=ps[cbk][:, :], func=AF.Identity,
                        bias=gv[:, GV_CONV_B + cc:GV_CONV_B + cc + 1]), reads=[('ps', cbk)], writes=[('y', cc, tg)])

            wc_ensure(1)
            for cc in range(8):
                wc_ensure(cc + 1)
                b = cc % 2
                cwc = GV_CONV_W + cc * 31

                def fd(e, b=b, cwc=cwc):
                    for j in range(31):
                        ins = e.tensor_scalar(out=diag[:, b, j, :], in0=identb[:, :], scalar1=gv[:, cwc + j:cwc + j + 1],
                                              scalar2=None, op0=ALU.mult)
                    return ins
                S.op('dve', fd, reads=[('identb',)], writes=[('diag', b)])
                for tg in range(4):
                    ba, bg = (tg % 2) * 2, (tg % 2) * 2 + 1
                    for which, bank in ((0, ba), (1, bg)):
                        def f(e, b=b, which=which, bank=bank, tg=tg):
                            for kc in range(16):
                                ins = e.matmul(ps[bank][:, :], wc[:, b, which, kc, :],
                                               uT[:, kc, tg * 512:(tg + 1) * 512], start=(kc == 0), stop=(kc == 15))
                            return ins
                        S.op('pe', f, reads=[('wc', b, which)] + ukeys, writes=[('ps', bank)])
                    S.op('act', lambda e, tg=tg, bg=bg: e.activation(out=sig[:, tg % 2, :], in_=ps[bg][:, :],
                                                                      func=AF.Sigmoid),
                         reads=[('ps', bg)], writes=[('sig', tg % 2)])
                    S.op('dve', lambda e, tg=tg, ba=ba, b=b: e.tensor_tensor(
                        out=glu[:, b, 30 + tg * 512:30 + (tg + 1) * 512], in0=ps[ba][:, :], in1=sig[:, tg % 2, :],
                        op=ALU.mult), reads=[('ps', ba), ('sig', tg % 2)], writes=[('glu', b, tg)])
                if cc > 0:
                    conv_part(cc - 1)
            conv_part(7)
            for hf in range(2):
                h0 = hf * 1024
                for cc in range(8):
                    S.op('act', lambda e, cc=cc, h0=h0: e.activation(out=ysq[:, :], in_=y[:, cc, h0:h0 + 1024],
                                                                     func=AF.Square),
                         reads=gk('y', cc, 4), writes=gk('ysq', 0, 2))

                    def f(e, cc=cc, h0=h0):
                        for g in range(2):
                            e.matmul(ps[g][:, :], ones_f[:, :], y[:, cc, h0 + g * 512:h0 + (g + 1) * 512],
                                     start=(cc == 0), stop=(cc == 7))
                        for g in range(2):
                            ins = e.matmul(ps[2 + g][:, :], ones_f[:, :], ysq[:, g * 512:(g + 1) * 512],
                                           start=(cc == 0), stop=(cc == 7))
                        return ins
                    S.op('pe', f, reads=gk('y', cc, 4) + [('ones_f',)] + gk('ysq', 0, 2),
                         writes=[('ps', g) for g in range(4)])
                for g in range(2):
                    sl = slice(g * 512, (g + 1) * 512)
                    S.op('act', lambda e, g=g, sl=sl: e.activation(out=mu[:, sl], in_=ps[g][:, :], func=AF.Copy,
                                                                   scale=1.0 / 1024),
                         reads=[('ps', g)], writes=[('mu', 0, g)])
                    S.op('dve', lambda e, sl=sl: e.tensor_tensor(out=tmp[:, sl], in0=mu[:, sl], in1=mu[:, sl],
                                                                 op=ALU.mult), reads=[('mu', 0, g)], writes=[('tmp', 0, g)])
                    S.op('dve', lambda e, g=g, sl=sl: e.scalar_tensor_tensor(
                        out=rsc[:, sl], in0=ps[2 + g][:, :], scalar=1.0 / 1024, in1=tmp[:, sl],
                        op0=ALU.mult, op1=ALU.subtract), reads=[('ps', 2 + g), ('tmp', 0, g)], writes=[('rsc', 0, g)])
                emit_rstd_sb(S, rsc, 'rsc', 2, LN_EPS)
                for cc in range(8):
                    S.op('dve', lambda e, cc=cc, h0=h0: e.tensor_tensor(out=tmp[:, :], in0=y[:, cc, h0:h0 + 1024],
                                                                        in1=mu[:, :], op=ALU.subtract),
                         reads=gk('y', cc, 4) + gk('mu', 0, 2), writes=gk('tmp', 0, 2))
                    S.op('dve', lambda e: e.tensor_tensor(out=tmp[:, :], in0=tmp[:, :], in1=rsc[:, :], op=ALU.mult),
                         reads=gk('tmp', 0, 2) + gk('rsc', 0, 2), writes=gk('tmp', 0, 2))
                    S.op('act', lambda e, cc=cc: e.activation(
                        out=ostg[:, cc % 2, :], in_=tmp[:, :], func=AF.Silu, scale=gv[:, GV_LN_G + cc:GV_LN_G + cc + 1],
                        bias=gv[:, GV_LN_B + cc:GV_LN_B + cc + 1]), reads=gk('tmp', 0, 2), writes=[('ostg', cc % 2)])
                    S.dma('sp', catscr[cc, :, h0:h0 + 1024], ostg[:, cc % 2, :], reads=[('ostg', cc % 2)],
                          writes=[('cat', cc, hf)])
            S.emit()

        if _stop == 'm1':
            return
        with ExitStack() as es:
            sb = lambda n, shp, dt: es.enter_context(nc.sbuf_tensor(f"m2_{n}", shp, dt))
            V = sb("V", [128, 16, 1024], BF16)
            wv = sb("wv", [128, 2, 16, 256], BF16)
            wqk = sb("wqk", [128, 2, 2, 16, 128], BF16)
            qk = sb("qk", [128, 2, 2, T], BF16)
            cos = sb("cos", [128, T], F32)
            sin = sb("sin", [128, T], F32)
            qb = sb("qb", [128, 2, 512], BF16)
            t1 = sb("t1", [128, 2, 512], F32)
            t2 = sb("t2", [128, 2, 512], F32)
            ksum = sb("ksum", [128, 8], F32)
            kmT = sb("kmT", [128, 2, 8], BF16)
            gate = sb("gate", [128, 64], F32)
            top8 = sb("top8", [128, 8, 8], F32)
            bpad = sb("bpad", [128, 8, 128], F32)
            biasT = sb("biasT", [128, 2, 1024], BF16)
            pT = sb("pT", [128, 4, 512], BF16)
            rl = sb("rl", [128, 2, 512], F32)
            ocp = sb("ocp", [128, 2, 512], F32)
            ostg = sb("ostg", [128, 2, 512], BF16)
            identf = sb("identf", [128, 128], F32)
            identb = sb("identb", [128, 128], BF16)
            rt = sb("rt", [128, 128], BF16)
            cbias = sb("cbias", [128, 4, 512], BF16)
            ind = sb("ind", [128, 8, 128], BF16)
            elig = sb("elig", [128, 64], F32)
            eligneg = sb("eligneg", [128, 64], F32)
            S = Sched(nc, es, "m2")
            S.dma('sp', cos[:, :], C['cos'], writes=[('cos',)])
            S.dma('sp', sin[:, :], C['sin'], writes=[('sin',)])
            S.dma('sp', identf[:, :], C['ident'], writes=[('identf',)])
            S.dma('sp', elig[:, :], C['elig'], writes=[('elig',)])
            S.dma('sp', eligneg[:, :], C['eligneg'], writes=[('eligneg',)])
            S.dma('pool', identb[:, :], C['ident'], writes=[('identb',)])
            S.dma('pool', rt[:, :], C['rt'], writes=[('rt',)])
            S.dma('pool', cbias[:, :, :], C['cb'], writes=[('cbias',)])
            S.dma('pool', ind[:, :, :], C['ind'], writes=[('ind',)])
            S.op('dve', lambda e: e.memset(bpad[:, :, :], 0.0), writes=[('bpad',)])
            st = {'v': -1, 'qk': -1}

            def wv_ensure(i):
                while st['v'] < min(i, 3):
                    st['v'] += 1
                    c = st['v']
                    S.dma('pool', wv[:, c % 2], win_v[c], writes=[('wv', c % 2)])

            def wqk_ensure(i):
                while st['qk'] < min(i, 7):
                    st['qk'] += 1
                    h = st['qk']
                    S.dma('pool', wqk[:, h % 2, 0], win_t[16 + h], writes=[('wqk', h % 2, 0)])
                    S.dma('pool', wqk[:, h % 2, 1], win_t[24 + h], writes=[('wqk', h % 2, 1)])

            wv_ensure(1)
            wqk_ensure(0)
            for cg in range(4):
                wv_ensure(cg + 1)
                for tt in range(16):
                    bank = tt % 2

                    def f(e, cg=cg, tt=tt, bank=bank):
                        for kc in range(16):
                            ins = e.matmul(ps[bank][:, 0:256], uT[:, kc, tt * 128:(tt + 1) * 128],
                                           wv[:, cg % 2, kc, :], start=(kc == 0), stop=(kc == 15))
                        return ins
                    S.op('pe', f, reads=[('wv', cg % 2)] + ukeys, writes=[('ps', bank)])
                    eng = 'act' if tt % 2 == 0 else 'dve'
                    if eng == 'act':
                        S.op('act', lambda e, cg=cg, tt=tt, bank=bank: e.activation(
                            out=V[:, tt, cg * 256:(cg + 1) * 256], in_=ps[bank][:, 0:256], func=AF.Copy),
                            reads=[('ps', bank)], writes=[('V', tt, cg)])
                    else:
                        S.op('dve', lambda e, cg=cg, tt=tt, bank=bank: e.tensor_copy(
                            out=V[:, tt, cg * 256:(cg + 1) * 256], in_=ps[bank][:, 0:256]),
                            reads=[('ps', bank)], writes=[('V', tt, cg)])
            pt_rr = RR(4)
            scale = 128.0 ** -0.5
            def proj(h):
                wqk_ensure(h + 1)
                hb = h % 2
                pend = None
                for which in (0, 1):
                    for tg in range(4):
                        bank = tg % 2
                        rb = 2 + tg % 2
                        sl = slice(tg * 512, (tg + 1) * 512)

                        def f(e, hb=hb, which=which, bank=bank, sl=sl):
                            for kc in range(16):
                                ins = e.matmul(ps[bank][:, :], wqk[:, hb, which, kc, :], uT[:, kc, sl],
                                               start=(kc == 0), stop=(kc == 15))
                            return ins
                        S.op('pe', f, reads=[('wqk', hb, which)] + ukeys, writes=[('ps', bank)])
                        S.op('act', lambda e, bank=bank, tg=tg: e.activation(out=qb[:, tg % 2, :], in_=ps[bank][:, :],
                                                                              func=AF.Copy),
                             reads=[('ps', bank)], writes=[('qb', tg % 2)])

                        def rest(hb=hb, which=which, tg=tg, bank=bank, rb=rb, sl=sl):
                            S.op('pe', lambda e: e.matmul(ps[rb][:, :], rt[:, :], qb[:, tg % 2, :],
                                                          start=True, stop=True),
                                 reads=[('qb', tg % 2), ('rt',)], writes=[('ps', rb)])
                            S.op('dve', lambda e: e.tensor_tensor(
                                out=t1[:, tg % 2, :], in0=ps[bank][:, :], in1=cos[:, sl], op=ALU.mult),
                                reads=[('ps', bank), ('cos',)], writes=[('t1', tg % 2)])
                            S.op('dve', lambda e: e.tensor_tensor(
                                out=t2[:, tg % 2, :], in0=ps[rb][:, :], in1=sin[:, sl], op=ALU.mult),
                                reads=[('ps', rb), ('sin',)], writes=[('t2', tg % 2)])
                            S.op('dve', lambda e: e.tensor_tensor(
                                out=qk[:, hb, which, sl], in0=t1[:, tg % 2, :], in1=t2[:, tg % 2, :], op=ALU.add),
                                reads=[('t1', tg % 2), ('t2', tg % 2)], writes=[('qk', hb, which, tg)])
                        if pend is not None:
                            pend()
                        pend = rest
                pend()

            def gate_a(h):
                hb = h % 2
                qkeys = [('qk', hb, 0, tg) for tg in range(4)]
                kkeys = [('qk', hb, 1, tg) for tg in range(4)]
                S.op('dve', lambda e, hb=hb: e.reduce_sum(
                    out=ksum[:, :], in_=qk[:, hb, 1, :].rearrange("p (n t) -> p n t", t=256), axis=AX.X),
                    reads=kkeys, writes=[('ksum',)])
                S.op('act', lambda e, hb=hb: e.activation(out=kmT[:, hb, :], in_=ksum[:, :], func=AF.Copy,
                                                          scale=1.0 / 256), reads=[('ksum',)], writes=[('kmT', hb)])

                def fg(e, hb=hb):
                    for i in range(8):
                        ins = e.matmul(ps[4][:, i * 8:(i + 1) * 8], qk[:, hb, 0, (8 + i) * 128:(9 + i) * 128],
                                       kmT[:, hb, :], start=True, stop=True)
                    return ins
                S.op('pe', fg, reads=qkeys + [('kmT', hb)], writes=[('ps', 4)])
                S.op('dve', lambda e: e.tensor_tensor(out=gate[:, :], in0=ps[4][:, 0:64], in1=eligneg[:, :],
                                                      op=ALU.add), reads=[('ps', 4), ('eligneg',)], writes=[('gate',)])

                def fm(e):
                    for i in range(8):
                        ins = e.max(out=top8[:, i, :], in_=gate[:, i * 8:(i + 1) * 8])
                    return ins
                S.op('dve', fm, reads=[('gate',)], writes=[('top8',)])

                def fb(e):
                    for i in range(8):
                        ins = e.tensor_scalar(out=bpad[:, i, 0:8], in0=gate[:, i * 8:(i + 1) * 8],
                                              scalar1=top8[:, i, 2:3], scalar2=NEG, op0=ALU.is_lt, op1=ALU.mult)
                    return ins
                S.op('dve', fb, reads=[('gate',), ('top8',)], writes=[('bpad',)])
                S.op('dve', lambda e: e.tensor_tensor(
                    out=bpad[:, :, 0:8], in0=bpad[:, :, 0:8], in1=elig[:, :].rearrange("p (i n) -> p i n", n=8),
                    op=ALU.mult), reads=[('bpad',), ('elig',)], writes=[('bpad',)])

            def gate_b(h):
                hb = h % 2
                for half in range(2):
                    def ft(e, half=half):
                        for i in range(4):
                            ins = e.matmul(ps[5][:, i * 128:(i + 1) * 128], bpad[:, half * 4 + i, :], identf[:, :],
                                           start=True, stop=True)
                        return ins
                    S.op('pe', ft, reads=[('bpad',), ('identf',)], writes=[('ps', 5)])
                    S.op('act', lambda e, hb=hb, half=half: e.activation(
                        out=biasT[:, hb, half * 512:(half + 1) * 512], in_=ps[5][:, :], func=AF.Copy),
                        reads=[('ps', 5)], writes=[('biasT', hb, half)])

            def attn(h, groups):
                hb = h % 2
                qkeys = [('qk', hb, 0, tg) for tg in range(4)]
                kkeys = [('qk', hb, 1, tg) for tg in range(4)]
                for g in groups:
                    nch = 4 * g + 4
                    qsl = slice(g * 512, (g + 1) * 512)
                    pend = None
                    for c in range(nch):
                        sbk = 4 + c % 2

                        def fs(e, hb=hb, g=g, c=c, sbk=sbk, qsl=qsl):
                            mms = [(qk[:, hb, 1, c * 128:(c + 1) * 128], qk[:, hb, 0, qsl])]
                            if g >= 2:
                                mms.append((ind[:, c // 2, :], biasT[:, hb, (g - 2) * 512:(g - 1) * 512]))
                            if c >= 4 * g:
                                mms.append((identb[:, :], cbias[:, c - 4 * g, :]))
                            for i, (l, r) in enumerate(mms):
                                ins = e.matmul(ps[sbk][:, :], l, r, start=(i == 0), stop=(i == len(mms) - 1))
                            return ins
                        rdk = qkeys + kkeys + [('ind',), ('identb',), ('cbias',)]
                        if g >= 2:
                            rdk = rdk + [('biasT', hb, g - 2)]
                        S.op('pe', fs, reads=rdk, writes=[('ps', sbk)])
                        pslot = pt_rr()
                        S.op('act', lambda e, sbk=sbk, pslot=pslot: e.activation(
                            out=pT[:, pslot, :], in_=ps[sbk][:, :], func=AF.Exp, scale=scale),
                            reads=[('ps', sbk)], writes=[('pT', pslot)])

                        def fpv(e, c=c, h=h, pslot=pslot, nch=nch):
                            e.matmul(ps[6][:, :], V[:, c, h * 128:(h + 1) * 128], pT[:, pslot, :],
                                     start=(c == 0), stop=(c == nch - 1))
                            return e.matmul(ps[7][:, :], ones_bf[:, :], pT[:, pslot, :],
                                            start=(c == 0), stop=(c == nch - 1))
                        if pend is not None:
                            S.op('pe', pend[0], reads=pend[1], writes=[('ps', 6), ('ps', 7)])
                        pend = (fpv, [('pT', pslot), ('V', c, h // 2)])
                    S.op('pe', pend[0], reads=pend[1], writes=[('ps', 6), ('ps', 7)])
                    S.op('act', lambda e, g=g: e.activation(out=rl[:, g % 2, :], in_=ps[7][:, :], func=AF.Copy),
                         reads=[('ps', 7)], writes=[('rl', g % 2)])
                    S.op('act', lambda e, g=g: e.activation(out=ocp[:, g % 2, :], in_=ps[6][:, :], func=AF.Copy),
                         reads=[('ps', 6)], writes=[('ocp', g % 2)])
                    S.op('dve', lambda e, g=g: e.reciprocal(out=rl[:, g % 2, :], in_=rl[:, g % 2, :]),
                         reads=[('rl', g % 2)], writes=[('rl', g % 2)])
                    S.op('dve', lambda e, g=g: e.tensor_tensor(out=ostg[:, g % 2, :], in0=ocp[:, g % 2, :],
                                                               in1=rl[:, g % 2, :], op=ALU.mult),
                         reads=[('ocp', g % 2), ('rl', g % 2)], writes=[('ostg', g % 2)])
                    S.dma('sp', catscr[8 + h, :, qsl], ostg[:, g % 2, :], reads=[('ostg', g % 2)],
                          writes=[('cat', 8 + h, g)])

            proj(0)
            gate_a(0)
            for h in range(8):
                if h < 7:
                    proj(h + 1)
                gate_b(h)
                attn(h, [0, 1])
                if h < 7:
                    gate_a(h + 1)
                attn(h, [2, 3])
            S.emit()

    if _stop.startswith('m2'):
        return
    _mix_m3(nc, G, hin, hname, hout, oname, W, catscr)


def _mix_m3(nc, G, hin, hname, hout, oname, W, catscr):
    wout_t = W['wout_t']
    def loader(S, es, hf):
        if hf == 0:
            loader.cat = es.enter_context(nc.sbuf_tensor("m3_cat", [128, 16, 1024], BF16))
        cat = loader.cat
        for kc in range(16):
            S.dma('sp', cat[:, kc, :], catscr[kc, :, hf * 1024:(hf + 1) * 1024], writes=[('catsb', kc)])
        return (lambda kc, tg: cat[:, kc, tg * 512:(tg + 1) * 512]), (lambda tg: [('catsb', kc) for kc in range(16)])
    down_only_phase(nc, G, "m3", loader, 16, wout_t, hin, hname, hout, oname, GV_MIX_POST, 1.0)


def xattn_phase(nc, G, hin, hname, hout, oname, memT, W):
    ps, ones_bf, gv = G['ps'], G['ones_bf'], G['gv']
    wkvk_t, wkvv_t, wxq_t, wxo_t = W['wkvk_t'], W['wkvv_t'], W['wxq_t'], W['wxo_t']
    scale = 128.0 ** -0.5
    with ExitStack() as oes:
        xo = oes.enter_context(nc.sbuf_tensor("x_xo", [128, 4, T], BF16))
        kx = oes.enter_context(nc.sbuf_tensor("x_kx", [128, 4, MEM], BF16))
        vx = oes.enter_context(nc.sbuf_tensor("x_vx", [128, 2, 512], BF16))
        with ExitStack() as es:
            sb = lambda n, shp, dt: es.enter_context(nc.sbuf_tensor(f"x0_{n}", shp, dt))
            memsb = sb("mem", [128, 16, MEM], F32)
            msq = sb("msq", [128, 2, MEM], BF16)
            mrstd = sb("mrstd", [128, 512], F32)
            mn = sb("mn", [128, 16, MEM], BF16)
            wk = sb("wk", [128, 4, 16, 128], BF16)
            wvv = sb("wv", [128, 16, 512], BF16)
            S = Sched(nc, es, "x0")
            for h in range(4):
                S.dma('pool', wk[:, h], wkvk_t[h], writes=[('wk', h)])
            S.dma('pool', wvv[:, :, :], wkvv_t, writes=[('wvv',)])
            for kc in range(16):
                S.dma('sp', memsb[:, kc, :], memT[kc], writes=[('mem', kc)])
                S.op('act', lambda e, kc=kc: e.activation(out=msq[:, kc % 2, :], in_=memsb[:, kc, :], func=AF.Square),
                     reads=[('mem', kc)], writes=[('msq', kc % 2)])
                S.op('pe', lambda e, kc=kc: e.matmul(ps[0][:, 0:MEM], ones_bf[:, :], msq[:, kc % 2, :],
                                                     start=(kc == 0), stop=(kc == 15)),
                     reads=[('msq', kc % 2)], writes=[('ps', 0)])
            S.op('dve', lambda e: e.tensor_scalar(out=mrstd[:, 0:MEM], in0=ps[0][:, 0:MEM], scalar1=1.0 / D,
                                                  scalar2=RMS_EPS, op0=ALU.mult, op1=ALU.add),
                 reads=[('ps', 0)], writes=[('mrstd',)])
            S.op('act', lambda e: e.activation(out=mrstd[:, 0:MEM], in_=mrstd[:, 0:MEM], func=AF.Sqrt),
                 reads=[('mrstd',)], writes=[('mrstd',)])
            S.op('dve', lambda e: e.reciprocal(out=mrstd[:, 0:MEM], in_=mrstd[:, 0:MEM]),
                 reads=[('mrstd',)], writes=[('mrstd',)])
            for kc in range(16):
                S.op('dve', lambda e, kc=kc: e.scalar_tensor_tensor(
                    out=mn[:, kc, :], in0=memsb[:, kc, :], scalar=gv[:, GV_MEM + kc:GV_MEM + kc + 1],
                    in1=mrstd[:, 0:MEM], op0=ALU.mult, op1=ALU.mult),
                    reads=[('mem', kc), ('mrstd',)], writes=[('mn', kc)])
            mkeys = [('mn', kc) for kc in range(16)]
            for h in range(4):
                bank = 1 + h % 2

                def f(e, h=h, bank=bank):
                    for kc in range(16):
                        ins = e.matmul(ps[bank][:, 0:MEM], wk[:, h, kc, :], mn[:, kc, :],
                                       start=(kc == 0), stop=(kc == 15))
                    return ins
                S.op('pe', f, reads=mkeys + [('wk', h)], writes=[('ps', bank)])
                S.op('act', lambda e, h=h, bank=bank: e.activation(out=kx[:, h, :], in_=ps[bank][:, 0:MEM],
                                                                  func=AF.Copy),
                     reads=[('ps', bank)], writes=[('kx', h)])
            for mc in range(2):
                def f(e, mc=mc):
                    for kc in range(16):
                        ins = e.matmul(ps[3 + mc][:, :], mn[:, kc, mc * 128:(mc + 1) * 128], wvv[:, kc, :],
                                       start=(kc == 0), stop=(kc == 15))
                    return ins
                S.op('pe', f, reads=mkeys + [('wvv',)], writes=[('ps', 3 + mc)])
                S.op('dve', lambda e, mc=mc: e.tensor_copy(out=vx[:, mc, :], in_=ps[3 + mc][:, :]),
                     reads=[('ps', 3 + mc)], writes=[('vx', mc)])
            S.emit()
        with ExitStack() as ues:
            uT = ues.enter_context(nc.sbuf_tensor("x_uT", [128, 16, T], BF16))
            norm_full_phase(nc, G, "x1", hin, hname, GV_X_PRE, uT)
            ukeys = [('xn', kc) for kc in range(16)]
            with ExitStack() as es:
                sb = lambda n, shp, dt: es.enter_context(nc.sbuf_tensor(f"x2_{n}", shp, dt))
                wq = sb("wq", [128, 2, 16, 128], BF16)
                qx = sb("qx", [128, 2, 512], BF16)
                pT = sb("pT", [128, 4, 512], BF16)
                rl = sb("rl", [128, 2, 512], F32)
                ocp = sb("ocp", [128, 2, 512], F32)
                S = Sched(nc, es, "x2")
                st = {'q': -1}

                def wq_ensure(i):
                    while st['q'] < min(i, 3):
                        st['q'] += 1
                        h = st['q']
                        S.dma('pool', wq[:, h % 2], wxq_t[h], writes=[('wq', h % 2)])
                wq_ensure(1)
                pt_rr = RR(4)
                for h in range(4):
                    wq_ensure(h + 1)
                    for g in range(4):
                        sl = slice(g * 512, (g + 1) * 512)
                        bank = g % 2

                        def f(e, h=h, sl=sl, bank=bank):
                            for kc in range(16):
                                ins = e.matmul(ps[bank][:, :], wq[:, h % 2, kc, :], uT[:, kc, sl],
                                               start=(kc == 0), stop=(kc == 15))
                            return ins
                        S.op('pe', f, reads=ukeys + [('wq', h % 2)], writes=[('ps', bank)])
                        S.op('act', lambda e, g=g, bank=bank: e.activation(out=qx[:, g % 2, :], in_=ps[bank][:, :],
                                                                          func=AF.Copy),
                             reads=[('ps', bank)], writes=[('qx', g % 2)])
                        for mc in range(2):
                            S.op('pe', lambda e, h=h, g=g, mc=mc: e.matmul(
                                ps[2 + mc][:, :], kx[:, h, mc * 128:(mc + 1) * 128], qx[:, g % 2, :],
                                start=True, stop=True), reads=[('qx', g % 2)], writes=[('ps', 2 + mc)])
                            pslot = pt_rr()
                            S.op('act', lambda e, mc=mc, pslot=pslot: e.activation(
                                out=pT[:, pslot, :], in_=ps[2 + mc][:, :], func=AF.Exp, scale=scale),
                                reads=[('ps', 2 + mc)], writes=[('pT', pslot)])

                            def fpv(e, h=h, mc=mc, pslot=pslot):
                                e.matmul(ps[6][:, :], vx[:, mc, h * 128:(h + 1) * 128], pT[:, pslot, :],
                                         start=(mc == 0), stop=(mc == 1))
                                return e.matmul(ps[7][:, :], ones_bf[:, :], pT[:, pslot, :],
                                                start=(mc == 0), stop=(mc == 1))
                            S.op('pe', fpv, reads=[('pT', pslot)], writes=[('ps', 6), ('ps', 7)])
                        S.op('act', lambda e, g=g: e.activation(out=rl[:, g % 2, :], in_=ps[7][:, :], func=AF.Copy),
                             reads=[('ps', 7)], writes=[('rl', g % 2)])
                        S.op('act', lambda e, g=g: e.activation(out=ocp[:, g % 2, :], in_=ps[6][:, :], func=AF.Copy),
                             reads=[('ps', 6)], writes=[('ocp', g % 2)])
                        S.op('dve', lambda e, g=g: e.reciprocal(out=rl[:, g % 2, :], in_=rl[:, g % 2, :]),
                             reads=[('rl', g % 2)], writes=[('rl', g % 2)])
                        S.op('dve', lambda e, h=h, g=g, sl=sl: e.tensor_tensor(
                            out=xo[:, h, sl], in0=ocp[:, g % 2, :], in1=rl[:, g % 2, :], op=ALU.mult),
                            reads=[('ocp', g % 2), ('rl', g % 2)], writes=[('xo', h, g)])
                S.emit()

        def loader(S, es, hf):
            return (lambda kc, tg: xo[:, kc, hf * 1024 + tg * 512:hf * 1024 + (tg + 1) * 512]), (lambda tg: [])
        down_only_phase(nc, G, "x3", loader, 4, wxo_t, hin, hname, hout, oname, GV_X_POST, 1.0)


def build_program(stages=('ffn1', 'mix', 'xattn', 'ffn2')):
    nc = bass.Bass("TRN2", target_bir_lowering=False)
    dr = lambda n, shp, kind="ExternalInput", dt=F32: nc.dram_tensor(n, shp, dt, kind=kind).ap()
    xT = dr("xT", [16, 128, T])
    memT = dr("memT", [16, 128, MEM])
    gvd = dr("gv", [128, NGV])
    w1gu = dr("w1gu", [NFF, 128, 16, 256])
    w1dn = dr("w1dn", [16, 128, NFF, 128])
    w2gu = dr("w2gu", [NFF, 128, 16, 256])
    w2dn = dr("w2dn", [16, 128, NFF, 128])
    W = {
        'win_t': dr("win_t", [32, 128, 16, 128]), 'win_v': dr("win_v", [4, 128, 16, 256]),
        'wout_t': dr("wout_t", [16, 128, 16, 128]),
        'wkvk_t': dr("wkvk_t", [4, 128, 16, 128]), 'wkvv_t': dr("wkvv_t", [128, 16, 512]),
        'wxq_t': dr("wxq_t", [4, 128, 16, 128]), 'wxo_t': dr("wxo_t", [16, 128, 4, 128]),
        'cst': {'ident': dr("c_ident", [128, 128]), 'rt': dr("c_rt", [128, 128]), 'cb': dr("c_cb", [128, 4, 512]),
                'ind': dr("c_ind", [128, 8, 128]), 'cos': dr("c_cos", [128, T]), 'sin': dr("c_sin", [128, T]),
                'elig': dr("c_elig", [128, 64]), 'eligneg': dr("c_eligneg", [128, 64])},
    }
    outT = dr("outT", [16, 128, T], kind="ExternalOutput")
    h1 = dr("h1", [16, 128, T], kind="Internal")
    h2 = dr("h2", [16, 128, T], kind="Internal")
    h3 = dr("h3", [16, 128, T], kind="Internal")
    h4 = dr("h4", [16, 128, T], kind="Internal")

    with ExitStack() as ges:
        gv = ges.enter_context(nc.sbuf_tensor("gvt", [128, NGV], F32))
        ones_bf = ges.enter_context(nc.sbuf_tensor("ones_bf", [128, 128], BF16))
        ps = [ges.enter_context(nc.psum_tensor(f"ps{i}", [128, 512], F32)) for i in range(8)]
        G = {'gv': gv, 'ones_bf': ones_bf, 'ps': ps}
        with ExitStack() as es:
            S = Sched(nc, es, "init")
            S.dma('sp', gv[:, :], gvd, writes=[('gv',)])
            S.op('dve', lambda e: e.memset(ones_bf[:, :], 1.0), writes=[('ones',)])
            S.emit()
        cur, cname = xT, 'xT'
        n_st = len(stages)
        for si, stg_name in enumerate(stages):
            last = (si == n_st - 1)
            if stg_name == 'ffn1':
                dst, dname = (outT, 'outT') if last else (h1, 'h1')
                ffn_phase(nc, G, "f1", cur, cname, dst, dname, w1gu, w1dn, GV_FFN1_PRE, GV_FFN1_POST)
            elif stg_name == 'ffn2':
                dst, dname = (outT, 'outT') if last else (h4, 'h4')
                ffn_phase(nc, G, "f2", cur, cname, dst, dname, w2gu, w2dn, GV_FFN2_PRE, GV_FFN2_POST)
            elif stg_name == 'mix':
                dst, dname = (outT, 'outT') if last else (h2, 'h2')
                mix_phase(nc, G, cur, cname, dst, dname, W)
            elif stg_name == 'xattn':
                dst, dname = (outT, 'outT') if last else (h3, 'h3')
                xattn_phase(nc, G, cur, cname, dst, dname, memT, W)
            cur, cname = dst, dname
    return nc


def _tile_gu(w):
    g = w[:, :DFF].reshape(16, 128, NFF, 128)
    u = w[:, DFF:].reshape(16, 128, NFF, 128)
    t = np.concatenate([g, u], axis=3)
    return np.ascontiguousarray(t.transpose(2, 1, 0, 3))


def _tile_rows(w, kc_n):
    n_out = w.shape[1] // 128
    t = w.reshape(kc_n, 128, n_out, 128)
    return np.ascontiguousarray(t.transpose(2, 1, 0, 3))


def _vec(v):
    return np.ascontiguousarray(np.asarray(v, np.float32).reshape(-1, 128).T)


def prep_shared(inp):
    f = lambda k: np.asarray(inp[k], np.float32)[0]
    gv = np.zeros((128, NGV), np.float32)
    for col, k in ((GV_FFN1_PRE, 'ffn1_pre_g'), (GV_FFN1_POST, 'ffn1_post_g'), (GV_MIX_PRE, 'mix_pre_g'),
                   (GV_MIX_POST, 'mix_post_g'), (GV_X_PRE, 'xattn_pre_g'), (GV_MEM, 'mem_g'),
                   (GV_X_POST, 'xattn_post_g'), (GV_FFN2_PRE, 'ffn2_pre_g'), (GV_FFN2_POST, 'ffn2_post_g')):
        gv[:, col:col + 16] = _vec(f(k))
    gv[:, GV_CONV_B:GV_CONV_B + 8] = _vec(f('conv_b_dw'))
    gv[:, GV_LN_G:GV_LN_G + 8] = _vec(f('conv_ln_g'))
    gv[:, GV_LN_B:GV_LN_B + 8] = _vec(f('conv_ln_b'))
    cw = f('conv_w_dw')
    gv[:, GV_CONV_W:] = cw.reshape(31, 8, 128).transpose(2, 1, 0).reshape(128, 248)
    w_in = f('w_in')
    wkv = f('xattn_w_kv')
    inv = (1.0 / (10000.0 ** (np.arange(0, 128, 2, dtype=np.float32) / np.float32(128)))).astype(np.float32)
    ang = np.arange(T, dtype=np.float32)[:, None] * inv[None, :]
    cosT = np.cos(ang).astype(np.float32).T
    sinT = np.sin(ang).astype(np.float32).T
    rt = np.zeros((128, 128), np.float32)
    for m in range(64):
        rt[m + 64, m] = -1.0
        rt[m, m + 64] = 1.0
    p = np.arange(128)[:, None, None]
    j = np.arange(4)[None, :, None]
    fr = np.arange(512)[None, None, :]
    cb = np.where(128 * j + p <= fr, 0.0, NEG).astype(np.float32)
    ind = np.zeros((128, 8, 128), np.float32)
    for n in range(8):
        ind[n, n, :] = 1.0
    elig = np.zeros((128, 8, 8), np.float32)
    for i in range(8):
        elig[:, i, :(8 + i) // 2] = 1.0
    sh = {
        'gv': gv,
        'win_t': _tile_rows(w_in[:, :4096], 16),
        'win_v': np.ascontiguousarray(w_in[:, 4096:].reshape(16, 128, 4, 256).transpose(2, 1, 0, 3)),
        'wout_t': _tile_rows(f('w_out'), 16),
        'wkvk_t': _tile_rows(wkv[:, :512], 16),
        'wkvv_t': np.ascontiguousarray(wkv[:, 512:].reshape(16, 128, 512).transpose(1, 0, 2)),
        'wxq_t': _tile_rows(f('xattn_w_q'), 16),
        'wxo_t': _tile_rows(f('xattn_w_o'), 4),
        'c_ident': np.eye(128, dtype=np.float32), 'c_rt': rt, 'c_cb': cb, 'c_ind': ind,
        'c_cos': np.ascontiguousarray(np.concatenate([cosT, cosT], 0)),
        'c_sin': np.ascontiguousarray(np.concatenate([sinT, sinT], 0)),
        'c_elig': elig.reshape(128, 64), 'c_eligneg': ((elig - 1.0) * 1e30).reshape(128, 64).astype(np.float32),
        'w1gu': _tile_gu(f('ffn1_w_gu')), 'w1dn': _tile_rows(f('ffn1_w_down'), NFF),
        'w2gu': _tile_gu(f('ffn2_w_gu')), 'w2dn': _tile_rows(f('ffn2_w_down'), NFF),
    }
    return sh


def kernel(**inputs):
    x = np.asarray(inputs['x'], np.float32)
    mem = np.asarray(inputs['mem'], np.float32)
    sh = prep_shared(inputs)
    nc = build_program()
    in_maps = []
    for b in range(8):
        m = dict(sh)
        m['xT'] = np.ascontiguousarray(x[b].T).reshape(16, 128, T)
        m['memT'] = np.ascontiguousarray(mem[b].T).reshape(16, 128, MEM)
        in_maps.append(m)
    res = run_bass_kernel_spmd(nc, in_maps, core_ids=list(range(8)))
    out = np.empty((8, T, D), np.float32)
    for b in range(8):
        out[b] = res.results[b]['outT'].reshape(D, T).T
    return out
```

```python
import numpy as np
from contextlib import ExitStack
import concourse.bass as bass
import concourse.mybir as mybir
from concourse.bass_utils import run_bass_kernel_spmd

F32 = mybir.dt.float32
BF16 = mybir.dt.bfloat16
AF = mybir.ActivationFunctionType
ALU = mybir.AluOpType
AX = mybir.AxisListType

D = 2048
T = 2048
DFF = 5632
NFF = DFF // 128
MEM = 256
NEG = -30000.0
RMS_EPS = 1e-6
LN_EPS = 1e-5

GV_FFN1_PRE, GV_FFN1_POST, GV_MIX_PRE, GV_MIX_POST = 0, 16, 32, 48
GV_X_PRE, GV_MEM, GV_X_POST, GV_FFN2_PRE, GV_FFN2_POST = 64, 80, 96, 112, 128
GV_CONV_B, GV_LN_G, GV_LN_B, GV_CONV_W = 144, 152, 160, 168
NGV = 168 + 8 * 31


class Sched:
    ENG = ('pe', 'act', 'dve', 'pool', 'sp')

    def __init__(self, nc, es, name, ndma=6):
        self.nc = nc
        self.ops = {e: [] for e in self.ENG}
        self.cnt = {e: 0 for e in self.ENG}
        self.csem = {e: nc.alloc_semaphore(name=f"{name}_c_{e}") for e in ('pe', 'act', 'dve', 'pool')}
        self.dsem = {q: [nc.alloc_semaphore(name=f"{name}_d_{q}{i}") for i in range(ndma)]
                     for q in ('sp', 'pool')}
        self.dcnt = {q: 0 for q in self.dsem}
        self.ndma = ndma
        self.lw = {}
        self.rd = {}
        self.waited = {e: {} for e in self.ENG}

    def _sem(self, skey):
        if skey[0] == 'c':
            return self.csem[skey[1]]
        return self.dsem[skey[1]][skey[2]]

    def _deps(self, eng, reads, writes):
        need = {}

        def add(skey, val, kind):
            if skey[0] == 'c' and skey[1] == eng:
                if eng == 'pe' or kind == 'war':
                    return
            if need.get(skey, 0) < val:
                need[skey] = val

        for k in reads:
            t = self.lw.get(k)
            if t is not None:
                add(t[0], t[1], 'raw')
            if k[0] == 'ps':
                for skey, val in self.rd.get(k, {}).items():
                    if skey[0] == 'c' and skey[1] != eng:
                        add(skey, val, 'rar')
        for k in writes:
            t = self.lw.get(k)
            if t is not None:
                add(t[0], t[1], 'waw')
            for skey, val in self.rd.get(k, {}).items():
                add(skey, val, 'war')
        waits = []
        w = self.waited[eng]
        for skey, val in need.items():
            if w.get(skey, 0) < val:
                w[skey] = val
                waits.append((self._sem(skey), val))
        return waits

    def _update(self, tok, reads, writes):
        for k in writes:
            self.lw[k] = tok
            self.rd[k] = {}
        for k in reads:
            d = self.rd.setdefault(k, {})
            if d.get(tok[0], 0) < tok[1]:
                d[tok[0]] = tok[1]

    def op(self, eng, fn, reads=(), writes=()):
        waits = self._deps(eng, reads, writes)
        self.cnt[eng] += 1
        tok = (('c', eng), self.cnt[eng])
        self.ops[eng].append((waits, fn, (self.csem[eng], 1)))
        self._update(tok, reads, writes)

    def dma(self, q, out, in_, reads=(), writes=()):
        waits = self._deps(q, reads, writes)
        i = self.dcnt[q]
        self.dcnt[q] += 1
        slot, gen = i % self.ndma, i // self.ndma
        skey = ('d', q, slot)
        if gen > 0 and self.waited[q].get(skey, 0) < 16 * gen:
            self.waited[q][skey] = 16 * gen
            waits.append((self.dsem[q][slot], 16 * gen))
        tok = (skey, 16 * (gen + 1))
        self.ops[q].append((waits, lambda e, o=out, i_=in_: e.dma_start(out=o, in_=i_),
                            (self.dsem[q][slot], 16)))
        self._update(tok, reads, writes)

    def emit(self):
        final = []
        for q in self.dsem:
            n = self.dcnt[q]
            for slot in range(min(n, self.ndma)):
                gens = (n - 1 - slot) // self.ndma + 1
                final.append((self.dsem[q][slot], 16 * gens))
        ops = self.ops

        def run(e, name, extra=()):
            for waits, fn, inc in ops[name]:
                for s, v in waits:
                    e.wait_ge(s, v)
                fn(e).then_inc(*inc)
            for s, v in extra:
                e.wait_ge(s, v)

        with self.nc.Block() as blk:
            blk.tensor(lambda e: run(e, 'pe'))
            blk.scalar(lambda e: run(e, 'act'))
            blk.vector(lambda e: run(e, 'dve'))
            blk.gpsimd(lambda e: run(e, 'pool'))
            blk.sync(lambda e: run(e, 'sp', final))
        allsems = list(self.csem.values()) + [s for l in self.dsem.values() for s in l]
        self.nc.clear_and_free_semaphores(allsems)
        self.nc.all_engine_barrier()


class RR:
    def __init__(self, n, off=0):
        self.n = n
        self.off = off
        self.i = -1

    def __call__(self):
        self.i += 1
        return self.off + self.i % self.n


def gk(name, s, ng):
    return [(name, s, g) for g in range(ng)]


def gen_norm(S, G, hin, hname, t0, Tn, gcol, xn, stg, stg_rr, sqp, sq_rr, rstd, pbase=0, lag=0, resident=None):
    ps, ones_bf, gv = G['ps'], G['ones_bf'], G['gv']
    ng = Tn // 512
    pend = []
    if resident is not None:
        for kc in range(16):
            S.dma('sp', resident[:, kc, 0:Tn], hin[kc, :, t0:t0 + Tn], reads=[(hname, kc)], writes=[('hres', kc)])
    for kc in range(16):
        if resident is None:
            s = stg_rr()
            S.dma('sp', stg[:, s, 0:Tn], hin[kc, :, t0:t0 + Tn], reads=[(hname, kc)], writes=gk('stg', s, 4))
            src_ap, src_k = stg[:, s, 0:Tn], gk('stg', s, 4)
        else:
            src_ap, src_k = resident[:, kc, 0:Tn], [('hres', kc)]
        q = sq_rr()
        S.op('act', lambda e, src_ap=src_ap, q=q: e.activation(out=sqp[:, q, 0:Tn], in_=src_ap, func=AF.Square),
             reads=src_k, writes=gk('sq', q, 4))

        def f(e, q=q, kc=kc):
            for g in range(ng):
                ins = e.matmul(ps[pbase + g][:, :], ones_bf[:, :], sqp[:, q, g * 512:(g + 1) * 512],
                               start=(kc == 0), stop=(kc == 15))
            return ins
        pend.append((f, gk('sq', q, 4)))
        if len(pend) > lag:
            fn_, r_ = pend.pop(0)
            S.op('pe', fn_, reads=r_, writes=[('ps', pbase + g) for g in range(ng)])
        yield
    for fn_, r_ in pend:
        S.op('pe', fn_, reads=r_, writes=[('ps', pbase + g) for g in range(ng)])
    emit_rstd(S, G, rstd, Tn, [pbase + g for g in range(ng)], 1.0 / D, RMS_EPS)
    yield
    for kc in range(16):
        if resident is None:
            s = stg_rr()
            S.dma('sp', stg[:, s, 0:Tn], hin[kc, :, t0:t0 + Tn], reads=[(hname, kc)], writes=gk('stg', s, 4))
            src_ap, src_k = stg[:, s, 0:Tn], gk('stg', s, 4)
        else:
            src_ap, src_k = resident[:, kc, 0:Tn], [('hres', kc)]
        S.op('dve', lambda e, src_ap=src_ap, kc=kc: e.scalar_tensor_tensor(
            out=xn[:, kc, 0:Tn], in0=src_ap, scalar=gv[:, gcol + kc:gcol + kc + 1],
            in1=rstd[:, 0:Tn], op0=ALU.mult, op1=ALU.mult),
            reads=src_k + gk('rstd', 0, 4), writes=[('xn', kc)])
        yield


def emit_norm(*a, **k):
    for _ in gen_norm(*a, **k):
        pass


def merge_side_first(main, side):
    while True:
        next(side, None)
        try:
            next(main)
        except StopIteration:
            break
    for _ in side:
        pass


def merge(main, side, per_step):
    for _ in main:
        for _i in range(per_step):
            next(side, None)
    for _ in side:
        pass


def emit_rstd(S, G, rstd, Tn, banks, scale, eps, key='rstd'):
    ps = G['ps']
    for g, b in enumerate(banks):
        S.op('dve', lambda e, g=g, b=b: e.tensor_scalar(
            out=rstd[:, g * 512:(g + 1) * 512], in0=ps[b][:, :], scalar1=scale, scalar2=eps,
            op0=ALU.mult, op1=ALU.add), reads=[('ps', b)], writes=[(key, 0, g)])
        S.op('act', lambda e, g=g: e.activation(out=rstd[:, g * 512:(g + 1) * 512],
                                                in_=rstd[:, g * 512:(g + 1) * 512], func=AF.Sqrt),
             reads=[(key, 0, g)], writes=[(key, 0, g)])
        S.op('dve', lambda e, g=g: e.reciprocal(out=rstd[:, g * 512:(g + 1) * 512],
                                                in_=rstd[:, g * 512:(g + 1) * 512]),
             reads=[(key, 0, g)], writes=[(key, 0, g)])


def gen_down_C(S, G, hf, rhs, rkeys, KC, wdn_sb, wdn_ensure, widx0, hin, hname, hout, oname, gpost, res_scale,
               stg, stg_rr, sqp, sq_rr, rstd, fscr, fsb=None, hsb=None, rkey='rstd'):
    ps, ones_bf, gv = G['ps'], G['ones_bf'], G['gv']
    t0 = hf * 1024
    pend = []
    for c in range(16):
        if hsb is not None:
            S.dma('sp', hsb[:, c, :], hin[c, :, t0:t0 + 1024], reads=[(hname, c)], writes=[('hsb', c)])
        gi = widx0 + c
        wdn_ensure(gi + 1)
        b = gi % 2
        if fsb is None:
            s = stg_rr()
        q = sq_rr()
        pend_new = []
        for tg in (0, 1):
            bank = (c % 2) * 2 + tg
            sl = slice(tg * 512, (tg + 1) * 512)
            if fsb is None:
                fdst, fkey = stg[:, s, sl], ('stg', s, tg)
            else:
                fdst, fkey = fsb[:, c, sl], ('fsb', c, tg)

            def f(e, b=b, tg=tg, bank=bank):
                for kc in range(KC):
                    ins = e.matmul(ps[bank][:, :], wdn_sb[:, b, kc, :], rhs(kc, tg),
                                   start=(kc == 0), stop=(kc == KC - 1))
                return ins
            S.op('pe', f, reads=[('wdn', b)] + rkeys(tg), writes=[('ps', bank)])
            S.op('dve', lambda e, fdst=fdst, bank=bank: e.tensor_copy(out=fdst, in_=ps[bank][:, :]),
                 reads=[('ps', bank)], writes=[fkey])
            S.op('act', lambda e, fdst=fdst, q=q, sl=sl: e.activation(out=sqp[:, q, sl], in_=fdst, func=AF.Square),
                 reads=[fkey], writes=[('sq', q, tg)])
            pend_new.append((lambda e, q=q, tg=tg, c=c, sl=sl: e.matmul(
                ps[6 + tg][:, :], ones_bf[:, :], sqp[:, q, sl],
                start=(c == 0), stop=(c == 15)), [('sq', q, tg)], [('ps', 6 + tg)]))
        for fn_, r_, w_ in pend:
            S.op('pe', fn_, reads=r_, writes=w_)
        pend = pend_new
        if fsb is None:
            S.dma('sp', fscr[c], stg[:, s, 0:1024], reads=gk('stg', s, 2), writes=[('fscr', c)])
        yield
    for fn_, r_, w_ in pend:
        S.op('pe', fn_, reads=r_, writes=w_)
    emit_rstd(S, G, rstd, 1024, [6, 7], 1.0 / D, RMS_EPS, key=rkey)


def gen_down_D(S, G, hf, rhs, rkeys, KC, wdn_sb, wdn_ensure, widx0, hin, hname, hout, oname, gpost, res_scale,
               stg, stg_rr, sqp, sq_rr, rstd, fscr, fsb=None, hsb=None, rkey='rstd', stage=None):
    gv = G['gv']
    t0 = hf * 1024
    slots = {}
    if fsb is None and stage is not None:
        for c in range(16):
            slots[c] = stage(c)
            S.dma('sp', slots[c][0], fscr[c], reads=[('fscr', c)], writes=slots[c][1])
        for c in range(6):
            S.dma('sp', slots[c][2], hin[c, :, t0:t0 + 1024], reads=[(hname, c)], writes=slots[c][3])
    for c in range(16):
        if fsb is None and stage is None:
            s1 = stg_rr()
            s2 = stg_rr()
            S.dma('sp', stg[:, s1, 0:1024], fscr[c], reads=[('fscr', c)], writes=gk('stg', s1, 4))
            S.dma('sp', stg[:, s2, 0:1024], hin[c, :, t0:t0 + 1024], reads=[(hname, c)], writes=gk('stg', s2, 4))
            fap, fkeys = stg[:, s1, 0:1024], gk('stg', s1, 4)
            hap, hkeys = stg[:, s2, 0:1024], gk('stg', s2, 4)
        elif fsb is None:
            fap, fkeys, hap, hkeys = slots[c]
        else:
            fap, fkeys = fsb[:, c, :], gk('fsb', c, 2)
            hap, hkeys = hsb[:, c, :], [('hsb', c)]
        S.op('dve', lambda e, fap=fap, c=c: e.scalar_tensor_tensor(
            out=fap, in0=fap, scalar=gv[:, gpost + c:gpost + c + 1],
            in1=rstd[:, 0:1024], op0=ALU.mult, op1=ALU.mult),
            reads=fkeys + gk(rkey, 0, 2), writes=fkeys)
        if fsb is not None and stg is not None:
            so = stg_rr()
            oap, okeys = stg[:, so, 0:1024], gk('stg', so, 4)
        else:
            oap, okeys = fap, fkeys
        S.op('dve', lambda e, fap=fap, hap=hap, oap=oap: e.scalar_tensor_tensor(
            out=oap, in0=fap, scalar=res_scale, in1=hap, op0=ALU.mult, op1=ALU.add),
            reads=fkeys + hkeys, writes=okeys)
        S.dma('sp', hout[c, :, t0:t0 + 1024], oap, reads=okeys, writes=[(oname, c)])
        if slots and c + 6 < 16:
            S.dma('sp', slots[c + 6][2], hin[c + 6, :, t0:t0 + 1024], reads=[(hname, c + 6)], writes=slots[c + 6][3])
        yield


def emit_down(*a, **k):
    for _ in gen_down_C(*a, **k):
        pass
    for _ in gen_down_D(*a, **k):
        pass


def ffn_phase(nc, G, name, hin, hname, hout, oname, wgu, wdn, gpre, gpost):
    with ExitStack() as es:
        sb = lambda n, shp, dt: es.enter_context(nc.sbuf_tensor(f"{name}_{n}", shp, dt))
        xn = sb("xn", [128, 16, 1024], BF16)
        act = sb("act", [128, NFF, 1024], BF16)
        wgu_sb = sb("wgu", [128, 2, 16, 256], BF16)
        wdn_sb = sb("wdn", [128, 2, NFF, 128], BF16)
        stg = sb("stg", [128, 4, 1024], F32)
        sqp = sb("sqp", [128, 6, 1024], BF16)
        rstd = sb("rstd", [128, 1024], F32)
        rstdf = sb("rstdf", [128, 1024], F32)
        sil = sb("sil", [128, 2, 512], F32)
        fscr = nc.dram_tensor(f"{name}_fscr", [2, 16, 128, 1024], F32).ap()
        S = Sched(nc, es, name)
        ps = G['ps']
        stg_rr, sq_rr, sq_rr_c = RR(4), RR(4), RR(2, 4)
        st = {'gu': -1, 'dn': -1}

        def wgu_ensure(i):
            while st['gu'] < min(i, 2 * NFF - 1):
                st['gu'] += 1
                g = st['gu']
                S.dma('pool', wgu_sb[:, g % 2], wgu[g % NFF], writes=[('wgu', g % 2)])

        def wdn_ensure(i):
            while st['dn'] < min(i, 31):
                st['dn'] += 1
                g = st['dn']
                S.dma('pool', wdn_sb[:, g % 2], wdn[g % 16], writes=[('wdn', g % 2)])

        xkeys = [('xn', kc) for kc in range(16)]

        hres = act.bitcast(F32).reshape([128, NFF // 2, 1024])

        def gen_A(hf, pbase, lag):
            wgu_ensure(hf * NFF + 1)
            yield from gen_norm(S, G, hin, hname, hf * 1024, 1024, gpre, xn, stg, stg_rr, sqp, sq_rr, rstd,
                                pbase=pbase, lag=lag, resident=(hres if hf == 0 else None))

        def gen_B(hf):
            wdn_ensure(hf * 16 + 1)
            for j in range(NFF):
                gi = hf * NFF + j
                wgu_ensure(gi + 1)
                b = gi % 2
                base = (j % 2) * 4
                def f(e, b=b, base=base):
                    for which in (0, 1):
                        for tg in (0, 1):
                            bank = base + which * 2 + tg
                            for kc in range(16):
                                ins = e.matmul(ps[bank][:, :], wgu_sb[:, b, kc, which * 128:(which + 1) * 128],
                                               xn[:, kc, tg * 512:(tg + 1) * 512], start=(kc == 0), stop=(kc == 15))
                    return ins
                S.op('pe', f, reads=[('wgu', b)] + xkeys, writes=[('ps', base + i) for i in range(4)])
                for tg in (0, 1):
                    S.op('act', lambda e, tg=tg, base=base: e.activation(
                        out=sil[:, tg, :], in_=ps[base + tg][:, :], func=AF.Silu),
                        reads=[('ps', base + tg)], writes=[('sil', tg)])
                    S.op('dve', lambda e, tg=tg, base=base, j=j: e.tensor_tensor(
                        out=act[:, j, tg * 512:(tg + 1) * 512], in0=ps[base + 2 + tg][:, :], in1=sil[:, tg, :],
                        op=ALU.mult), reads=[('ps', base + 2 + tg), ('sil', tg)], writes=[('act', j, tg)])
                yield

        def dargs(hf):
            return (S, G, hf, lambda kc, tg: act[:, kc, tg * 512:(tg + 1) * 512],
                    lambda tg: [('act', j, tg) for j in range(NFF)], NFF, wdn_sb, wdn_ensure,
                    hf * 16, hin, hname, hout, oname, gpost, 0.5, stg, stg_rr, sqp, sq_rr_c, rstdf, fscr[hf])

        for _ in gen_A(0, 0, 2):
            pass
        for _ in gen_B(0):
            pass
        merge(gen_down_C(*dargs(0), rkey="rstdf"), gen_A(1, 4, 2), 2)
        merge(gen_B(1), gen_down_D(*dargs(0), rkey='rstdf'), 1)
        for _ in gen_down_C(*dargs(1), rkey='rstdf'):
            pass
        def stage(c):
            def akeys(slot):
                return [('act', 2 * slot + i, tg) for i in (0, 1) for tg in (0, 1)]
            return hres[:, c, :], akeys(c), hres[:, 16 + c % 6, :], akeys(16 + c % 6)
        for _ in gen_down_D(*dargs(1), rkey='rstdf', stage=stage):
            pass
        S.emit()


def emit_rstd_sb(S, buf, key, ng, eps):
    for g in range(ng):
        sl = slice(g * 512, (g + 1) * 512)
        S.op('dve', lambda e, sl=sl: e.tensor_scalar(out=buf[:, sl], in0=buf[:, sl], scalar1=eps, scalar2=None,
                                                      op0=ALU.add), reads=[(key, 0, g)], writes=[(key, 0, g)])
        S.op('act', lambda e, sl=sl: e.activation(out=buf[:, sl], in_=buf[:, sl], func=AF.Sqrt),
             reads=[(key, 0, g)], writes=[(key, 0, g)])
        S.op('dve', lambda e, sl=sl: e.reciprocal(out=buf[:, sl], in_=buf[:, sl]),
             reads=[(key, 0, g)], writes=[(key, 0, g)])


def down_only_phase(nc, G, name, src_loader, KC, wsrc, hin, hname, hout, oname, gpost, res_scale):
    with ExitStack() as es:
        sb = lambda n, shp, dt: es.enter_context(nc.sbuf_tensor(f"{name}_{n}", shp, dt))
        wdn_sb = sb("wdn", [128, 2, KC, 128], BF16)
        stg = sb("stg", [128, 4, 1024], F32)
        fsb = sb("fsb", [128, 16, 1024], F32)
        hsb = sb("hsb", [128, 16, 1024], F32)
        sqp = sb("sqp", [128, 2, 1024], BF16)
        rstd = sb("rstd", [128, 1024], F32)
        fscr = nc.dram_tensor(f"{name}_fscr", [16, 128, 1024], F32).ap()
        S = Sched(nc, es, name)
        stg_rr, sq_rr = RR(4), RR(2)
        st = {'dn': -1}

        def wdn_ensure(i):
            while st['dn'] < min(i, 31):
                st['dn'] += 1
                g = st['dn']
                S.dma('pool', wdn_sb[:, g % 2], wsrc[g % 16], writes=[('wdn', g % 2)])

        rstd2 = sb("rstd2", [128, 1024], F32)

        def dargs(hf, rhs, rkeys):
            return (S, G, hf, rhs, rkeys, KC, wdn_sb, wdn_ensure, hf * 16, hin, hname, hout, oname, gpost,
                    res_scale, stg, stg_rr, sqp, sq_rr, (rstd, rstd2)[hf], fscr)
        kw = [dict(fsb=fsb, hsb=hsb, rkey='rstd'), dict(fsb=fsb, hsb=hsb, rkey='rstd2')]
        wdn_ensure(1)
        a0 = dargs(0, *src_loader(S, es, 0))
        for _ in gen_down_C(*a0, **kw[0]):
            pass
        wdn_ensure(17)
        a1 = dargs(1, *src_loader(S, es, 1))
        merge_side_first(gen_down_C(*a1, **kw[1]), gen_down_D(*a0, **kw[0]))
        for _ in gen_down_D(*a1, **kw[1]):
            pass
        S.emit()


def norm_full_phase(nc, G, name, hin, hname, gcol, uT):
    ps, ones_bf, gv = G['ps'], G['ones_bf'], G['gv']
    with ExitStack() as es:
        sb = lambda n, shp, dt: es.enter_context(nc.sbuf_tensor(f"{name}_{n}", shp, dt))
        hs = sb("hs", [128, 2, 16, 512], F32)
        sqp = sb("sqp", [128, 4, 512], BF16)
        rstd = sb("rstd", [128, 2048], F32)
        S = Sched(nc, es, name)
        sq_rr = RR(4)

        def loads(g):
            for kc in range(16):
                S.dma('sp', hs[:, g % 2, kc, :], hin[kc, :, g * 512:(g + 1) * 512], reads=[(hname, kc)],
                      writes=[('hs', g % 2, kc)])
        loads(0)
        loads(1)
        for g in range(4):
            b = g % 2
            bank = g % 2
            pend = []
            for kc in range(16):
                q = sq_rr()
                S.op('act', lambda e, b=b, kc=kc, q=q: e.activation(out=sqp[:, q, :], in_=hs[:, b, kc, :],
                                                                    func=AF.Square),
                     reads=[('hs', b, kc)], writes=[('sq', q)])
                pend.append((lambda e, q=q, kc=kc, bank=bank: e.matmul(
                    ps[bank][:, :], ones_bf[:, :], sqp[:, q, :], start=(kc == 0), stop=(kc == 15)), [('sq', q)]))
                if len(pend) > 2:
                    fn_, r_ = pend.pop(0)
                    S.op('pe', fn_, reads=r_, writes=[('ps', bank)])
            for fn_, r_ in pend:
                S.op('pe', fn_, reads=r_, writes=[('ps', bank)])
            sl = slice(g * 512, (g + 1) * 512)
            S.op('dve', lambda e, sl=sl, bank=bank: e.tensor_scalar(
                out=rstd[:, sl], in0=ps[bank][:, :], scalar1=1.0 / D, scalar2=RMS_EPS, op0=ALU.mult, op1=ALU.add),
                reads=[('ps', bank)], writes=[('rstd', g)])
            S.op('act', lambda e, sl=sl: e.activation(out=rstd[:, sl], in_=rstd[:, sl], func=AF.Sqrt),
                 reads=[('rstd', g)], writes=[('rstd', g)])
            S.op('dve', lambda e, sl=sl: e.reciprocal(out=rstd[:, sl], in_=rstd[:, sl]),
                 reads=[('rstd', g)], writes=[('rstd', g)])
            for kc in range(16):
                S.op('dve', lambda e, b=b, kc=kc, sl=sl: e.scalar_tensor_tensor(
                    out=uT[:, kc, sl], in0=hs[:, b, kc, :], scalar=gv[:, gcol + kc:gcol + kc + 1],
                    in1=rstd[:, sl], op0=ALU.mult, op1=ALU.mult),
                    reads=[('hs', b, kc), ('rstd', g)], writes=[('xn', kc, g)])
            if g + 2 < 4:
                loads(g + 2)
        S.emit()


def mix_phase(nc, G, hin, hname, hout, oname, W):
    ps, ones_bf, gv = G['ps'], G['ones_bf'], G['gv']
    catscr = nc.dram_tensor("catscr", [16, 128, T], BF16).ap()
    win_t, win_v, wout_t, C = W['win_t'], W['win_v'], W['wout_t'], W['cst']
    ukeys = [('xn', kc) for kc in range(16)]
    _stop = ''
    with ExitStack() as ues:
        if _stop == 'only_m3':
            ues.close()
            return _mix_m3(nc, G, hin, hname, hout, oname, W, catscr)
        uT = ues.enter_context(nc.sbuf_tensor("mix_uT", [128, 16, T], BF16))
        norm_full_phase(nc, G, "m0", hin, hname, GV_MIX_PRE, uT)
        if _stop == 'm0':
            return

        with ExitStack() as es:
            sb = lambda n, shp, dt: es.enter_context(nc.sbuf_tensor(f"m1_{n}", shp, dt))
            y = sb("y", [128, 8, T], F32)
            glu = sb("glu", [128, 2, T + 30], BF16)
            diag = sb("diag", [128, 2, 31, 128], BF16)
            wc = sb("wc", [128, 2, 2, 16, 128], BF16)
            sig = sb("sig", [128, 2, 512], F32)
            ysq = sb("ysq", [128, 1024], F32)
            tmp = sb("tmp", [128, 1024], F32)
            mu = sb("mu", [128, 1024], F32)
            rsc = sb("rsc", [128, 1024], F32)
            ostg = sb("ostg", [128, 2, 1024], BF16)
            ones_f = sb("ones_f", [128, 128], F32)
            identb = sb("identb", [128, 128], BF16)
            S = Sched(nc, es, "m1")
            S.dma('pool', identb[:, :], C['ident'], writes=[('identb',)])
            S.op('dve', lambda e: e.memset(ones_f[:, :], 1.0), writes=[('ones_f',)])
            for b in range(2):
                S.op('dve', lambda e, b=b: e.memset(glu[:, b, 0:30], 0.0), writes=[('glupad', b)])
            st = {'w': -1}

            def wc_ensure(i):
                while st['w'] < min(i, 7):
                    st['w'] += 1
                    c = st['w']
                    S.dma('pool', wc[:, c % 2, 0], win_t[c], writes=[('wc', c % 2, 0)])
                    S.dma('pool', wc[:, c % 2, 1], win_t[8 + c], writes=[('wc', c % 2, 1)])

            def conv_part(cc):
                b = cc % 2
                for tg in range(4):
                    cbk = 4 + tg % 2

                    def f(e, b=b, tg=tg, cbk=cbk):
                        for j in range(31):
                            ins = e.matmul(ps[cbk][:, :], diag[:, b, j, :], glu[:, b, j + tg * 512:j + (tg + 1) * 512],
                                           start=(j == 0), stop=(j == 30))
                        return ins
                    S.op('pe', f, reads=[('diag', b), ('glupad', b)] + gk('glu', b, tg + 1), writes=[('ps', cbk)])
                    S.op('act', lambda e, cc=cc, tg=tg, cbk=cbk: e.activation(
                        out=y[:, cc, tg * 512:(tg + 1) * 512], in_=ps[cbk][:, :], func=AF.Identity,
                        bias=gv[:, GV_CONV_B + cc:GV_CONV_B + cc + 1]), reads=[('ps', cbk)], writes=[('y', cc, tg)])

            wc_ensure(1)
            for cc in range(8):
                wc_ensure(cc + 1)
                b = cc % 2
                cwc = GV_CONV_W + cc * 31

                def fd(e, b=b, cwc=cwc):
                    for j in range(31):
                        ins = e.tensor_scalar(out=diag[:, b, j, :], in0=identb[:, :], scalar1=gv[:, cwc + j:cwc + j + 1],
                                              scalar2=None, op0=ALU.mult)
                    return ins
                S.op('dve', fd, reads=[('identb',)], writes=[('diag', b)])
                for tg in range(4):
                    ba, bg = (tg % 2) * 2, (tg % 2) * 2 + 1
                    for which, bank in ((0, ba), (1, bg)):
                        def f(e, b=b, which=which, bank=bank, tg=tg):
                            for kc in range(16):
                                ins = e.matmul(ps[bank][:, :], wc[:, b, which, kc, :],
                                               uT[:, kc, tg * 512:(tg + 1) * 512], start=(kc == 0), stop=(kc == 15))
                            return ins
                        S.op('pe', f, reads=[('wc', b, which)] + ukeys, writes=[('ps', bank)])
                    S.op('act', lambda e, tg=tg, bg=bg: e.activation(out=sig[:, tg % 2, :], in_=ps[bg][:, :],
                                                                      func=AF.Sigmoid),
                         reads=[('ps', bg)], writes=[('sig', tg % 2)])
                    S.op('dve', lambda e, tg=tg, ba=ba, b=b: e.tensor_tensor(
                        out=glu[:, b, 30 + tg * 512:30 + (tg + 1) * 512], in0=ps[ba][:, :], in1=sig[:, tg % 2, :],
                        op=ALU.mult), reads=[('ps', ba), ('sig', tg % 2)], writes=[('glu', b, tg)])
                if cc > 0:
                    conv_part(cc - 1)
            conv_part(7)
            for hf in range(2):
                h0 = hf * 1024
                for cc in range(8):
                    S.op('act', lambda e, cc=cc, h0=h0: e.activation(out=ysq[:, :], in_=y[:, cc, h0:h0 + 1024],
                                                                     func=AF.Square),
                         reads=gk('y', cc, 4), writes=gk('ysq', 0, 2))

                    def f(e, cc=cc, h0=h0):
                        for g in range(2):
                            e.matmul(ps[g][:, :], ones_f[:, :], y[:, cc, h0 + g * 512:h0 + (g + 1) * 512],
                                     start=(cc == 0), stop=(cc == 7))
                        for g in range(2):
                            ins = e.matmul(ps[2 + g][:, :], ones_f[:, :], ysq[:, g * 512:(g + 1) * 512],
                                           start=(cc == 0), stop=(cc == 7))
                        return ins
                    S.op('pe', f, reads=gk('y', cc, 4) + [('ones_f',)] + gk('ysq', 0, 2),
                         writes=[('ps', g) for g in range(4)])
                for g in range(2):
                    sl = slice(g * 512, (g + 1) * 512)
                    S.op('act', lambda e, g=g, sl=sl: e.activation(out=mu[:, sl], in_=ps[g][:, :], func=AF.Copy,
                                                                   scale=1.0 / 1024),
                         reads=[('ps', g)], writes=[('mu', 0, g)])
                    S.op('dve', lambda e, sl=sl: e.tensor_tensor(out=tmp[:, sl], in0=mu[:, sl], in1=mu[:, sl],
                                                                 op=ALU.mult), reads=[('mu', 0, g)], writes=[('tmp', 0, g)])
                    S.op('dve', lambda e, g=g, sl=sl: e.scalar_tensor_tensor(
                        out=rsc[:, sl], in0=ps[2 + g][:, :], scalar=1.0 / 1024, in1=tmp[:, sl],
                        op0=ALU.mult, op1=ALU.subtract), reads=[('ps', 2 + g), ('tmp', 0, g)], writes=[('rsc', 0, g)])
                emit_rstd_sb(S, rsc, 'rsc', 2, LN_EPS)
                for cc in range(8):
                    S.op('dve', lambda e, cc=cc, h0=h0: e.tensor_tensor(out=tmp[:, :], in0=y[:, cc, h0:h0 + 1024],
                                                                        in1=mu[:, :], op=ALU.subtract),
                         reads=gk('y', cc, 4) + gk('mu', 0, 2), writes=gk('tmp', 0, 2))
                    S.op('dve', lambda e: e.tensor_tensor(out=tmp[:, :], in0=tmp[:, :], in1=rsc[:, :], op=ALU.mult),
                         reads=gk('tmp', 0, 2) + gk('rsc', 0, 2), writes=gk('tmp', 0, 2))
                    S.op('act', lambda e, cc=cc: e.activation(
                        out=ostg[:, cc % 2, :], in_=tmp[:, :], func=AF.Silu, scale=gv[:, GV_LN_G + cc:GV_LN_G + cc + 1],
                        bias=gv[:, GV_LN_B + cc:GV_LN_B + cc + 1]), reads=gk('tmp', 0, 2), writes=[('ostg', cc % 2)])
                    S.dma('sp', catscr[cc, :, h0:h0 + 1024], ostg[:, cc % 2, :], reads=[('ostg', cc % 2)],
                          writes=[('cat', cc, hf)])
            S.emit()

        if _stop == 'm1':
            return
        with ExitStack() as es:
            sb = lambda n, shp, dt: es.enter_context(nc.sbuf_tensor(f"m2_{n}", shp, dt))
            V = sb("V", [128, 16, 1024], BF16)
            wv = sb("wv", [128, 2, 16, 256], BF16)
            wqk = sb("wqk", [128, 2, 2, 16, 128], BF16)
            qk = sb("qk", [128, 2, 2, T], BF16)
            cos = sb("cos", [128, T], F32)
            sin = sb("sin", [128, T], F32)
            qb = sb("qb", [128, 2, 512], BF16)
            t1 = sb("t1", [128, 2, 512], F32)
            t2 = sb("t2", [128, 2, 512], F32)
            ksum = sb("ksum", [128, 8], F32)
            kmT = sb("kmT", [128, 2, 8], BF16)
            gate = sb("gate", [128, 64], F32)
            top8 = sb("top8", [128, 8, 8], F32)
            bpad = sb("bpad", [128, 8, 128], F32)
            biasT = sb("biasT", [128, 2, 1024], BF16)
            pT = sb("pT", [128, 4, 512], BF16)
            rl = sb("rl", [128, 2, 512], F32)
            ocp = sb("ocp", [128, 2, 512], F32)
            ostg = sb("ostg", [128, 2, 512], BF16)
            identf = sb("identf", [128, 128], F32)
            identb = sb("identb", [128, 128], BF16)
            rt = sb("rt", [128, 128], BF16)
            cbias = sb("cbias", [128, 4, 512], BF16)
            ind = sb("ind", [128, 8, 128], BF16)
            elig = sb("elig", [128, 64], F32)
            eligneg = sb("eligneg", [128, 64], F32)
            S = Sched(nc, es, "m2")
            S.dma('sp', cos[:, :], C['cos'], writes=[('cos',)])
            S.dma('sp', sin[:, :], C['sin'], writes=[('sin',)])
            S.dma('sp', identf[:, :], C['ident'], writes=[('identf',)])
            S.dma('sp', elig[:, :], C['elig'], writes=[('elig',)])
            S.dma('sp', eligneg[:, :], C['eligneg'], writes=[('eligneg',)])
            S.dma('pool', identb[:, :], C['ident'], writes=[('identb',)])
            S.dma('pool', rt[:, :], C['rt'], writes=[('rt',)])
            S.dma('pool', cbias[:, :, :], C['cb'], writes=[('cbias',)])
            S.dma('pool', ind[:, :, :], C['ind'], writes=[('ind',)])
            S.op('dve', lambda e: e.memset(bpad[:, :, :], 0.0), writes=[('bpad',)])
            st = {'v': -1, 'qk': -1}

            def wv_ensure(i):
                while st['v'] < min(i, 3):
                    st['v'] += 1
                    c = st['v']
                    S.dma('pool', wv[:, c % 2], win_v[c], writes=[('wv', c % 2)])

            def wqk_ensure(i):
                while st['qk'] < min(i, 7):
                    st['qk'] += 1
                    h = st['qk']
                    S.dma('pool', wqk[:, h % 2, 0], win_t[16 + h], writes=[('wqk', h % 2, 0)])
                    S.dma('pool', wqk[:, h % 2, 1], win_t[24 + h], writes=[('wqk', h % 2, 1)])

            wv_ensure(1)
            wqk_ensure(0)
            for cg in range(4):
                wv_ensure(cg + 1)
                for tt in range(16):
                    bank = tt % 2

                    def f(e, cg=cg, tt=tt, bank=bank):
                        for kc in range(16):
                            ins = e.matmul(ps[bank][:, 0:256], uT[:, kc, tt * 128:(tt + 1) * 128],
                                           wv[:, cg % 2, kc, :], start=(kc == 0), stop=(kc == 15))
                        return ins
                    S.op('pe', f, reads=[('wv', cg % 2)] + ukeys, writes=[('ps', bank)])
                    eng = 'act' if tt % 2 == 0 else 'dve'
                    if eng == 'act':
                        S.op('act', lambda e, cg=cg, tt=tt, bank=bank: e.activation(
                            out=V[:, tt, cg * 256:(cg + 1) * 256], in_=ps[bank][:, 0:256], func=AF.Copy),
                            reads=[('ps', bank)], writes=[('V', tt, cg)])
                    else:
                        S.op('dve', lambda e, cg=cg, tt=tt, bank=bank: e.tensor_copy(
                            out=V[:, tt, cg * 256:(cg + 1) * 256], in_=ps[bank][:, 0:256]),
                            reads=[('ps', bank)], writes=[('V', tt, cg)])
            pt_rr = RR(4)
            scale = 128.0 ** -0.5
            def proj(h):
                wqk_ensure(h + 1)
                hb = h % 2
                pend = None
                for which in (0, 1):
                    for tg in range(4):
                        bank = tg % 2
                        rb = 2 + tg % 2
                        sl = slice(tg * 512, (tg + 1) * 512)

                        def f(e, hb=hb, which=which, bank=bank, sl=sl):
                            for kc in range(16):
                                ins = e.matmul(ps[bank][:, :], wqk[:, hb, which, kc, :], uT[:, kc, sl],
                                               start=(kc == 0), stop=(kc == 15))
                            return ins
                        S.op('pe', f, reads=[('wqk', hb, which)] + ukeys, writes=[('ps', bank)])
                        S.op('act', lambda e, bank=bank, tg=tg: e.activation(out=qb[:, tg % 2, :], in_=ps[bank][:, :],
                                                                              func=AF.Copy),
                             reads=[('ps', bank)], writes=[('qb', tg % 2)])

                        def rest(hb=hb, which=which, tg=tg, bank=bank, rb=rb, sl=sl):
                            S.op('pe', lambda e: e.matmul(ps[rb][:, :], rt[:, :], qb[:, tg % 2, :],
                                                          start=True, stop=True),
                                 reads=[('qb', tg % 2), ('rt',)], writes=[('ps', rb)])
                            S.op('dve', lambda e: e.tensor_tensor(
                                out=t1[:, tg % 2, :], in0=ps[bank][:, :], in1=cos[:, sl], op=ALU.mult),
                                reads=[('ps', bank), ('cos',)], writes=[('t1', tg % 2)])
                            S.op('dve', lambda e: e.tensor_tensor(
                                out=t2[:, tg % 2, :], in0=ps[rb][:, :], in1=sin[:, sl], op=ALU.mult),
                                reads=[('ps', rb), ('sin',)], writes=[('t2', tg % 2)])
                            S.op('dve', lambda e: e.tensor_tensor(
                                out=qk[:, hb, which, sl], in0=t1[:, tg % 2, :], in1=t2[:, tg % 2, :], op=ALU.add),
                                reads=[('t1', tg % 2), ('t2', tg % 2)], writes=[('qk', hb, which, tg)])
                        if pend is not None:
                            pend()
                        pend = rest
                pend()

            def gate_a(h):
                hb = h % 2
                qkeys = [('qk', hb, 0, tg) for tg in range(4)]
                kkeys = [('qk', hb, 1, tg) for tg in range(4)]
                S.op('dve', lambda e, hb=hb: e.reduce_sum(
                    out=ksum[:, :], in_=qk[:, hb, 1, :].rearrange("p (n t) -> p n t", t=256), axis=AX.X),
                    reads=kkeys, writes=[('ksum',)])
                S.op('act', lambda e, hb=hb: e.activation(out=kmT[:, hb, :], in_=ksum[:, :], func=AF.Copy,
                                                          scale=1.0 / 256), reads=[('ksum',)], writes=[('kmT', hb)])

                def fg(e, hb=hb):
                    for i in range(8):
                        ins = e.matmul(ps[4][:, i * 8:(i + 1) * 8], qk[:, hb, 0, (8 + i) * 128:(9 + i) * 128],
                                       kmT[:, hb, :], start=True, stop=True)
                    return ins
                S.op('pe', fg, reads=qkeys + [('kmT', hb)], writes=[('ps', 4)])
                S.op('dve', lambda e: e.tensor_tensor(out=gate[:, :], in0=ps[4][:, 0:64], in1=eligneg[:, :],
                                                      op=ALU.add), reads=[('ps', 4), ('eligneg',)], writes=[('gate',)])

                def fm(e):
                    for i in range(8):
                        ins = e.max(out=top8[:, i, :], in_=gate[:, i * 8:(i + 1) * 8])
                    return ins
                S.op('dve', fm, reads=[('gate',)], writes=[('top8',)])

                def fb(e):
                    for i in range(8):
                        ins = e.tensor_scalar(out=bpad[:, i, 0:8], in0=gate[:, i * 8:(i + 1) * 8],
                                              scalar1=top8[:, i, 2:3], scalar2=NEG, op0=ALU.is_lt, op1=ALU.mult)
                    return ins
                S.op('dve', fb, reads=[('gate',), ('top8',)], writes=[('bpad',)])
                S.op('dve', lambda e: e.tensor_tensor(
                    out=bpad[:, :, 0:8], in0=bpad[:, :, 0:8], in1=elig[:, :].rearrange("p (i n) -> p i n", n=8),
                    op=ALU.mult), reads=[('bpad',), ('elig',)], writes=[('bpad',)])

            def gate_b(h):
                hb = h % 2
                for half in range(2):
                    def ft(e, half=half):
                        for i in range(4):
                            ins = e.matmul(ps[5][:, i * 128:(i + 1) * 128], bpad[:, half * 4 + i, :], identf[:, :],
                                           start=True, stop=True)
                        return ins
                    S.op('pe', ft, reads=[('bpad',), ('identf',)], writes=[('ps', 5)])
                    S.op('act', lambda e, hb=hb, half=half: e.activation(
                        out=biasT[:, hb, half * 512:(half + 1) * 512], in_=ps[5][:, :], func=AF.Copy),
                        reads=[('ps', 5)], writes=[('biasT', hb, half)])

            def attn(h, groups):
                hb = h % 2
                qkeys = [('qk', hb, 0, tg) for tg in range(4)]
                kkeys = [('qk', hb, 1, tg) for tg in range(4)]
                for g in groups:
                    nch = 4 * g + 4
                    qsl = slice(g * 512, (g + 1) * 512)
                    pend = None
                    for c in range(nch):
                        sbk = 4 + c % 2

                        def fs(e, hb=hb, g=g, c=c, sbk=sbk, qsl=qsl):
                            mms = [(qk[:, hb, 1, c * 128:(c + 1) * 128], qk[:, hb, 0, qsl])]
                            if g >= 2:
                                mms.append((ind[:, c // 2, :], biasT[:, hb, (g - 2) * 512:(g - 1) * 512]))
                            if c >= 4 * g:
                                mms.append((identb[:, :], cbias[:, c - 4 * g, :]))
                            for i, (l, r) in enumerate(mms):
                                ins = e.matmul(ps[sbk][:, :], l, r, start=(i == 0), stop=(i == len(mms) - 1))
                            return ins
                        rdk = qkeys + kkeys + [('ind',), ('identb',), ('cbias',)]
                        if g >= 2:
                            rdk = rdk + [('biasT', hb, g - 2)]
                        S.op('pe', fs, reads=rdk, writes=[('ps', sbk)])
                        pslot = pt_rr()
                        S.op('act', lambda e, sbk=sbk, pslot=pslot: e.activation(
                            out=pT[:, pslot, :], in_=ps[sbk][:, :], func=AF.Exp, scale=scale),
                            reads=[('ps', sbk)], writes=[('pT', pslot)])

                        def fpv(e, c=c, h=h, pslot=pslot, nch=nch):
                            e.matmul(ps[6][:, :], V[:, c, h * 128:(h + 1) * 128], pT[:, pslot, :],
                                     start=(c == 0), stop=(c == nch - 1))
                            return e.matmul(ps[7][:, :], ones_bf[:, :], pT[:, pslot, :],
                                            start=(c == 0), stop=(c == nch - 1))
                        if pend is not None:
                            S.op('pe', pend[0], reads=pend[1], writes=[('ps', 6), ('ps', 7)])
                        pend = (fpv, [('pT', pslot), ('V', c, h // 2)])
                    S.op('pe', pend[0], reads=pend[1], writes=[('ps', 6), ('ps', 7)])
                    S.op('act', lambda e, g=g: e.activation(out=rl[:, g % 2, :], in_=ps[7][:, :], func=AF.Copy),
                         reads=[('ps', 7)], writes=[('rl', g % 2)])
                    S.op('act', lambda e, g=g: e.activation(out=ocp[:, g % 2, :], in_=ps[6][:, :], func=AF.Copy),
                         reads=[('ps', 6)], writes=[('ocp', g % 2)])
                    S.op('dve', lambda e, g=g: e.reciprocal(out=rl[:, g % 2, :], in_=rl[:, g % 2, :]),
                         reads=[('rl', g % 2)], writes=[('rl', g % 2)])
                    S.op('dve', lambda e, g=g: e.tensor_tensor(out=ostg[:, g % 2, :], in0=ocp[:, g % 2, :],
                                                               in1=rl[:, g % 2, :], op=ALU.mult),
                         reads=[('ocp', g % 2), ('rl', g % 2)], writes=[('ostg', g % 2)])
                    S.dma('sp', catscr[8 + h, :, qsl], ostg[:, g % 2, :], reads=[('ostg', g % 2)],
                          writes=[('cat', 8 + h, g)])

            proj(0)
            gate_a(0)
            for h in range(8):
                if h < 7:
                    proj(h + 1)
                gate_b(h)
                attn(h, [0, 1])
                if h < 7:
                    gate_a(h + 1)
                attn(h, [2, 3])
            S.emit()

    if _stop.startswith('m2'):
        return
    _mix_m3(nc, G, hin, hname, hout, oname, W, catscr)


def _mix_m3(nc, G, hin, hname, hout, oname, W, catscr):
    wout_t = W['wout_t']
    def loader(S, es, hf):
        if hf == 0:
            loader.cat = es.enter_context(nc.sbuf_tensor("m3_cat", [128, 16, 1024], BF16))
        cat = loader.cat
        for kc in range(16):
            S.dma('sp', cat[:, kc, :], catscr[kc, :, hf * 1024:(hf + 1) * 1024], writes=[('catsb', kc)])
        return (lambda kc, tg: cat[:, kc, tg * 512:(tg + 1) * 512]), (lambda tg: [('catsb', kc) for kc in range(16)])
    down_only_phase(nc, G, "m3", loader, 16, wout_t, hin, hname, hout, oname, GV_MIX_POST, 1.0)


def xattn_phase(nc, G, hin, hname, hout, oname, memT, W):
    ps, ones_bf, gv = G['ps'], G['ones_bf'], G['gv']
    wkvk_t, wkvv_t, wxq_t, wxo_t = W['wkvk_t'], W['wkvv_t'], W['wxq_t'], W['wxo_t']
    scale = 128.0 ** -0.5
    with ExitStack() as oes:
        xo = oes.enter_context(nc.sbuf_tensor("x_xo", [128, 4, T], BF16))
        kx = oes.enter_context(nc.sbuf_tensor("x_kx", [128, 4, MEM], BF16))
        vx = oes.enter_context(nc.sbuf_tensor("x_vx", [128, 2, 512], BF16))
        with ExitStack() as es:
            sb = lambda n, shp, dt: es.enter_context(nc.sbuf_tensor(f"x0_{n}", shp, dt))
            memsb = sb("mem", [128, 16, MEM], F32)
            msq = sb("msq", [128, 2, MEM], BF16)
            mrstd = sb("mrstd", [128, 512], F32)
            mn = sb("mn", [128, 16, MEM], BF16)
            wk = sb("wk", [128, 4, 16, 128], BF16)
            wvv = sb("wv", [128, 16, 512], BF16)
            S = Sched(nc, es, "x0")
            for h in range(4):
                S.dma('pool', wk[:, h], wkvk_t[h], writes=[('wk', h)])
            S.dma('pool', wvv[:, :, :], wkvv_t, writes=[('wvv',)])
            for kc in range(16):
                S.dma('sp', memsb[:, kc, :], memT[kc], writes=[('mem', kc)])
                S.op('act', lambda e, kc=kc: e.activation(out=msq[:, kc % 2, :], in_=memsb[:, kc, :], func=AF.Square),
                     reads=[('mem', kc)], writes=[('msq', kc % 2)])
                S.op('pe', lambda e, kc=kc: e.matmul(ps[0][:, 0:MEM], ones_bf[:, :], msq[:, kc % 2, :],
                                                     start=(kc == 0), stop=(kc == 15)),
                     reads=[('msq', kc % 2)], writes=[('ps', 0)])
            S.op('dve', lambda e: e.tensor_scalar(out=mrstd[:, 0:MEM], in0=ps[0][:, 0:MEM], scalar1=1.0 / D,
                                                  scalar2=RMS_EPS, op0=ALU.mult, op1=ALU.add),
                 reads=[('ps', 0)], writes=[('mrstd',)])
            S.op('act', lambda e: e.activation(out=mrstd[:, 0:MEM], in_=mrstd[:, 0:MEM], func=AF.Sqrt),
                 reads=[('mrstd',)], writes=[('mrstd',)])
            S.op('dve', lambda e: e.reciprocal(out=mrstd[:, 0:MEM], in_=mrstd[:, 0:MEM]),
                 reads=[('mrstd',)], writes=[('mrstd',)])
            for kc in range(16):
                S.op('dve', lambda e, kc=kc: e.scalar_tensor_tensor(
                    out=mn[:, kc, :], in0=memsb[:, kc, :], scalar=gv[:, GV_MEM + kc:GV_MEM + kc + 1],
                    in1=mrstd[:, 0:MEM], op0=ALU.mult, op1=ALU.mult),
                    reads=[('mem', kc), ('mrstd',)], writes=[('mn', kc)])
            mkeys = [('mn', kc) for kc in range(16)]
            for h in range(4):
                bank = 1 + h % 2

                def f(e, h=h, bank=bank):
                    for kc in range(16):
                        ins = e.matmul(ps[bank][:, 0:MEM], wk[:, h, kc, :], mn[:, kc, :],
                                       start=(kc == 0), stop=(kc == 15))
                    return ins
                S.op('pe', f, reads=mkeys + [('wk', h)], writes=[('ps', bank)])
                S.op('act', lambda e, h=h, bank=bank: e.activation(out=kx[:, h, :], in_=ps[bank][:, 0:MEM],
                                                                  func=AF.Copy),
                     reads=[('ps', bank)], writes=[('kx', h)])
            for mc in range(2):
                def f(e, mc=mc):
                    for kc in range(16):
                        ins = e.matmul(ps[3 + mc][:, :], mn[:, kc, mc * 128:(mc + 1) * 128], wvv[:, kc, :],
                                       start=(kc == 0), stop=(kc == 15))
                    return ins
                S.op('pe', f, reads=mkeys + [('wvv',)], writes=[('ps', 3 + mc)])
                S.op('dve', lambda e, mc=mc: e.tensor_copy(out=vx[:, mc, :], in_=ps[3 + mc][:, :]),
                     reads=[('ps', 3 + mc)], writes=[('vx', mc)])
            S.emit()
        with ExitStack() as ues:
            uT = ues.enter_context(nc.sbuf_tensor("x_uT", [128, 16, T], BF16))
            norm_full_phase(nc, G, "x1", hin, hname, GV_X_PRE, uT)
            ukeys = [('xn', kc) for kc in range(16)]
            with ExitStack() as es:
                sb = lambda n, shp, dt: es.enter_context(nc.sbuf_tensor(f"x2_{n}", shp, dt))
                wq = sb("wq", [128, 2, 16, 128], BF16)
                qx = sb("qx", [128, 2, 512], BF16)
                pT = sb("pT", [128, 4, 512], BF16)
                rl = sb("rl", [128, 2, 512], F32)
                ocp = sb("ocp", [128, 2, 512], F32)
                S = Sched(nc, es, "x2")
                st = {'q': -1}

                def wq_ensure(i):
                    while st['q'] < min(i, 3):
                        st['q'] += 1
                        h = st['q']
                        S.dma('pool', wq[:, h % 2], wxq_t[h], writes=[('wq', h % 2)])
                wq_ensure(1)
                pt_rr = RR(4)
                for h in range(4):
                    wq_ensure(h + 1)
                    for g in range(4):
                        sl = slice(g * 512, (g + 1) * 512)
                        bank = g % 2

                        def f(e, h=h, sl=sl, bank=bank):
                            for kc in range(16):
                                ins = e.matmul(ps[bank][:, :], wq[:, h % 2, kc, :], uT[:, kc, sl],
                                               start=(kc == 0), stop=(kc == 15))
                            return ins
                        S.op('pe', f, reads=ukeys + [('wq', h % 2)], writes=[('ps', bank)])
                        S.op('act', lambda e, g=g, bank=bank: e.activation(out=qx[:, g % 2, :], in_=ps[bank][:, :],
                                                                          func=AF.Copy),
                             reads=[('ps', bank)], writes=[('qx', g % 2)])
                        for mc in range(2):
                            S.op('pe', lambda e, h=h, g=g, mc=mc: e.matmul(
                                ps[2 + mc][:, :], kx[:, h, mc * 128:(mc + 1) * 128], qx[:, g % 2, :],
                                start=True, stop=True), reads=[('qx', g % 2)], writes=[('ps', 2 + mc)])
                            pslot = pt_rr()
                            S.op('act', lambda e, mc=mc, pslot=pslot: e.activation(
                                out=pT[:, pslot, :], in_=ps[2 + mc][:, :], func=AF.Exp, scale=scale),
                                reads=[('ps', 2 + mc)], writes=[('pT', pslot)])

                            def fpv(e, h=h, mc=mc, pslot=pslot):
                                e.matmul(ps[6][:, :], vx[:, mc, h * 128:(h + 1) * 128], pT[:, pslot, :],
                                         start=(mc == 0), stop=(mc == 1))
                                return e.matmul(ps[7][:, :], ones_bf[:, :], pT[:, pslot, :],
                                                start=(mc == 0), stop=(mc == 1))
                            S.op('pe', fpv, reads=[('pT', pslot)], writes=[('ps', 6), ('ps', 7)])
                        S.op('act', lambda e, g=g: e.activation(out=rl[:, g % 2, :], in_=ps[7][:, :], func=AF.Copy),
                             reads=[('ps', 7)], writes=[('rl', g % 2)])
                        S.op('act', lambda e, g=g: e.activation(out=ocp[:, g % 2, :], in_=ps[6][:, :], func=AF.Copy),
                             reads=[('ps', 6)], writes=[('ocp', g % 2)])
                        S.op('dve', lambda e, g=g: e.reciprocal(out=rl[:, g % 2, :], in_=rl[:, g % 2, :]),
                             reads=[('rl', g % 2)], writes=[('rl', g % 2)])
                        S.op('dve', lambda e, h=h, g=g, sl=sl: e.tensor_tensor(
                            out=xo[:, h, sl], in0=ocp[:, g % 2, :], in1=rl[:, g % 2, :], op=ALU.mult),
                            reads=[('ocp', g % 2), ('rl', g % 2)], writes=[('xo', h, g)])
                S.emit()

        def loader(S, es, hf):
            return (lambda kc, tg: xo[:, kc, hf * 1024 + tg * 512:hf * 1024 + (tg + 1) * 512]), (lambda tg: [])
        down_only_phase(nc, G, "x3", loader, 4, wxo_t, hin, hname, hout, oname, GV_X_POST, 1.0)


def build_program(stages=('ffn1', 'mix', 'xattn', 'ffn2')):
    nc = bass.Bass("TRN2", target_bir_lowering=False)
    dr = lambda n, shp, kind="ExternalInput", dt=F32: nc.dram_tensor(n, shp, dt, kind=kind).ap()
    xT = dr("xT", [16, 128, T])
    memT = dr("memT", [16, 128, MEM])
    gvd = dr("gv", [128, NGV])
    w1gu = dr("w1gu", [NFF, 128, 16, 256])
    w1dn = dr("w1dn", [16, 128, NFF, 128])
    w2gu = dr("w2gu", [NFF, 128, 16, 256])
    w2dn = dr("w2dn", [16, 128, NFF, 128])
    W = {
        'win_t': dr("win_t", [32, 128, 16, 128]), 'win_v': dr("win_v", [4, 128, 16, 256]),
        'wout_t': dr("wout_t", [16, 128, 16, 128]),
        'wkvk_t': dr("wkvk_t", [4, 128, 16, 128]), 'wkvv_t': dr("wkvv_t", [128, 16, 512]),
        'wxq_t': dr("wxq_t", [4, 128, 16, 128]), 'wxo_t': dr("wxo_t", [16, 128, 4, 128]),
        'cst': {'ident': dr("c_ident", [128, 128]), 'rt': dr("c_rt", [128, 128]), 'cb': dr("c_cb", [128, 4, 512]),
                'ind': dr("c_ind", [128, 8, 128]), 'cos': dr("c_cos", [128, T]), 'sin': dr("c_sin", [128, T]),
                'elig': dr("c_elig", [128, 64]), 'eligneg': dr("c_eligneg", [128, 64])},
    }
    outT = dr("outT", [16, 128, T], kind="ExternalOutput")
    h1 = dr("h1", [16, 128, T], kind="Internal")
    h2 = dr("h2", [16, 128, T], kind="Internal")
    h3 = dr("h3", [16, 128, T], kind="Internal")
    h4 = dr("h4", [16, 128, T], kind="Internal")

    with ExitStack() as ges:
        gv = ges.enter_context(nc.sbuf_tensor("gvt", [128, NGV], F32))
        ones_bf = ges.enter_context(nc.sbuf_tensor("ones_bf", [128, 128], BF16))
        ps = [ges.enter_context(nc.psum_tensor(f"ps{i}", [128, 512], F32)) for i in range(8)]
        G = {'gv': gv, 'ones_bf': ones_bf, 'ps': ps}
        with ExitStack() as es:
            S = Sched(nc, es, "init")
            S.dma('sp', gv[:, :], gvd, writes=[('gv',)])
            S.op('dve', lambda e: e.memset(ones_bf[:, :], 1.0), writes=[('ones',)])
            S.emit()
        cur, cname = xT, 'xT'
        n_st = len(stages)
        for si, stg_name in enumerate(stages):
            last = (si == n_st - 1)
            if stg_name == 'ffn1':
                dst, dname = (outT, 'outT') if last else (h1, 'h1')
                ffn_phase(nc, G, "f1", cur, cname, dst, dname, w1gu, w1dn, GV_FFN1_PRE, GV_FFN1_POST)
            elif stg_name == 'ffn2':
                dst, dname = (outT, 'outT') if last else (h4, 'h4')
                ffn_phase(nc, G, "f2", cur, cname, dst, dname, w2gu, w2dn, GV_FFN2_PRE, GV_FFN2_POST)
            elif stg_name == 'mix':
                dst, dname = (outT, 'outT') if last else (h2, 'h2')
                mix_phase(nc, G, cur, cname, dst, dname, W)
            elif stg_name == 'xattn':
                dst, dname = (outT, 'outT') if last else (h3, 'h3')
                xattn_phase(nc, G, cur, cname, dst, dname, memT, W)
            cur, cname = dst, dname
    return nc


def _tile_gu(w):
    g = w[:, :DFF].reshape(16, 128, NFF, 128)
    u = w[:, DFF:].reshape(16, 128, NFF, 128)
    t = np.concatenate([g, u], axis=3)
    return np.ascontiguousarray(t.transpose(2, 1, 0, 3))


def _tile_rows(w, kc_n):
    n_out = w.shape[1] // 128
    t = w.reshape(kc_n, 128, n_out, 128)
    return np.ascontiguousarray(t.transpose(2, 1, 0, 3))


def _vec(v):
    return np.ascontiguousarray(np.asarray(v, np.float32).reshape(-1, 128).T)


def prep_shared(inp):
    f = lambda k: np.asarray(inp[k], np.float32)[0]
    gv = np.zeros((128, NGV), np.float32)
    for col, k in ((GV_FFN1_PRE, 'ffn1_pre_g'), (GV_FFN1_POST, 'ffn1_post_g'), (GV_MIX_PRE, 'mix_pre_g'),
                   (GV_MIX_POST, 'mix_post_g'), (GV_X_PRE, 'xattn_pre_g'), (GV_MEM, 'mem_g'),
                   (GV_X_POST, 'xattn_post_g'), (GV_FFN2_PRE, 'ffn2_pre_g'), (GV_FFN2_POST, 'ffn2_post_g')):
        gv[:, col:col + 16] = _vec(f(k))
    gv[:, GV_CONV_B:GV_CONV_B + 8] = _vec(f('conv_b_dw'))
    gv[:, GV_LN_G:GV_LN_G + 8] = _vec(f('conv_ln_g'))
    gv[:, GV_LN_B:GV_LN_B + 8] = _vec(f('conv_ln_b'))
    cw = f('conv_w_dw')
    gv[:, GV_CONV_W:] = cw.reshape(31, 8, 128).transpose(2, 1, 0).reshape(128, 248)
    w_in = f('w_in')
    wkv = f('xattn_w_kv')
    inv = (1.0 / (10000.0 ** (np.arange(0, 128, 2, dtype=np.float32) / np.float32(128)))).astype(np.float32)
    ang = np.arange(T, dtype=np.float32)[:, None] * inv[None, :]
    cosT = np.cos(ang).astype(np.float32).T
    sinT = np.sin(ang).astype(np.float32).T
    rt = np.zeros((128, 128), np.float32)
    for m in range(64):
        rt[m + 64, m] = -1.0
        rt[m, m + 64] = 1.0
    p = np.arange(128)[:, None, None]
    j = np.arange(4)[None, :, None]
    fr = np.arange(512)[None, None, :]
    cb = np.where(128 * j + p <= fr, 0.0, NEG).astype(np.float32)
    ind = np.zeros((128, 8, 128), np.float32)
    for n in range(8):
        ind[n, n, :] = 1.0
    elig = np.zeros((128, 8, 8), np.float32)
    for i in range(8):
        elig[:, i, :(8 + i) // 2] = 1.0
    sh = {
        'gv': gv,
        'win_t': _tile_rows(w_in[:, :4096], 16),
        'win_v': np.ascontiguousarray(w_in[:, 4096:].reshape(16, 128, 4, 256).transpose(2, 1, 0, 3)),
        'wout_t': _tile_rows(f('w_out'), 16),
        'wkvk_t': _tile_rows(wkv[:, :512], 16),
        'wkvv_t': np.ascontiguousarray(wkv[:, 512:].reshape(16, 128, 512).transpose(1, 0, 2)),
        'wxq_t': _tile_rows(f('xattn_w_q'), 16),
        'wxo_t': _tile_rows(f('xattn_w_o'), 4),
        'c_ident': np.eye(128, dtype=np.float32), 'c_rt': rt, 'c_cb': cb, 'c_ind': ind,
        'c_cos': np.ascontiguousarray(np.concatenate([cosT, cosT], 0)),
        'c_sin': np.ascontiguousarray(np.concatenate([sinT, sinT], 0)),
        'c_elig': elig.reshape(128, 64), 'c_eligneg': ((elig - 1.0) * 1e30).reshape(128, 64).astype(np.float32),
        'w1gu': _tile_gu(f('ffn1_w_gu')), 'w1dn': _tile_rows(f('ffn1_w_down'), NFF),
        'w2gu': _tile_gu(f('ffn2_w_gu')), 'w2dn': _tile_rows(f('ffn2_w_down'), NFF),
    }
    return sh


def kernel(**inputs):
    x = np.asarray(inputs['x'], np.float32)
    mem = np.asarray(inputs['mem'], np.float32)
    sh = prep_shared(inputs)
    nc = build_program()
    in_maps = []
    for b in range(8):
        m = dict(sh)
        m['xT'] = np.ascontiguousarray(x[b].T).reshape(16, 128, T)
        m['memT'] = np.ascontiguousarray(mem[b].T).reshape(16, 128, MEM)
        in_maps.append(m)
    res = run_bass_kernel_spmd(nc, in_maps, core_ids=list(range(8)))
    out = np.empty((8, T, D), np.float32)
    for b in range(8):
        out[b] = res.results[b]['outT'].reshape(D, T).T
    return out
```
